# Optimizing a Trainium2 kernel written in Bass

```python
import math, functools
import jax, jax.numpy as jnp
from jax import lax
import numpy as np

D_MODEL = 1024
BATCH = 8
SEQ = 8192
DEPTH = 1
DEC_BATCH = 8
DEC_SEQ = 16
PAST_LEN = 1024

CHUNK = 64
N_HEADS_A = 8
HEAD_DIM_A = 64
V_DIM_A = 2 * HEAD_DIM_A
Q_WIDTH = N_HEADS_A * 2 * HEAD_DIM_A
ATTN_WIDTH = N_HEADS_A * V_DIM_A
GMLP_CHUNK = 128
GMLP_GROUPS = 8
GMLP_WIDTH = 1024
GMLP_GROUP_DIM = GMLP_WIDTH // GMLP_GROUPS
IN_WIDTH = 2 * Q_WIDTH + ATTN_WIDTH + 2 * GMLP_WIDTH
D_FF = 2816
Q_BLOCK = 128
N_MODS = 9
EPS = 1e-6

kernel_name = "streaming_diffattn_gmlp_macaron_adaln"


def lambda_init_of(layer_idx):
    return 0.8 - 0.6 * math.exp(-0.3 * layer_idx)


def rms_norm(x, g):
    xf = x.astype(jnp.float32)
    y = xf * lax.rsqrt(jnp.mean(xf * xf, axis=-1, keepdims=True) + EPS)
    return (y * g.astype(jnp.float32)).astype(x.dtype)


def swiglu(h, w_gu, w_down):
    gu = h @ w_gu
    return (jax.nn.silu(gu[..., :D_FF]) * gu[..., D_FF:]) @ w_down


def alibi_slopes():
    return jnp.asarray(2.0 ** (-8.0 * np.arange(1, N_HEADS_A + 1) / N_HEADS_A), dtype=jnp.float32)


def alibi_chunk_bias(q_pos, k_pos):
    dist = jnp.abs(q_pos[:, None] - k_pos[None, :]).astype(jnp.float32)
    visible = (k_pos[None, :] // CHUNK) <= (q_pos[:, None] // CHUNK)
    bias = -alibi_slopes()[:, None, None] * dist[None]
    return jnp.where(visible[None], bias, -jnp.inf)


def diff_attn_core(q, k, v, q_pos, k_pos, lam):
    s = jnp.einsum('bqhmd,bkhmd->bhmqk', q, k).astype(jnp.float32) * (HEAD_DIM_A ** -0.5)
    s = s + alibi_chunk_bias(q_pos, k_pos)[None, :, None]
    p = jax.nn.softmax(s, axis=-1)
    a = p[:, :, 0] - lam * p[:, :, 1]
    return jnp.einsum('bhqk,bkhe->bqhe', a.astype(v.dtype), v)


def attend_prompt(q, k, v, lam):
    b, t = q.shape[0], q.shape[1]
    nb = t // Q_BLOCK
    qb = jnp.moveaxis(q.reshape(b, nb, Q_BLOCK, N_HEADS_A, 2, HEAD_DIM_A), 1, 0)
    pos = jnp.arange(t, dtype=jnp.int32)
    posb = pos.reshape(nb, Q_BLOCK)

    def one_block(args):
        qi, pi = args
        return diff_attn_core(qi, k, v, pi, pos, lam)

    o = lax.map(one_block, (qb, posb))
    return jnp.moveaxis(o, 0, 1).reshape(b, t, N_HEADS_A, V_DIM_A)


def attend_sample(q, k, v, lam, ck, cv):
    past = ck.shape[1]
    t = q.shape[1]
    k_all = jnp.concatenate([ck.astype(k.dtype), k], axis=1)
    v_all = jnp.concatenate([cv.astype(v.dtype), v], axis=1)
    k_pos = jnp.arange(past + t, dtype=jnp.int32)
    q_pos = past + jnp.arange(t, dtype=jnp.int32)
    return diff_attn_core(q, k_all, v_all, q_pos, k_pos, lam)


def spatial_gate(u, gv, w_s, b_s):
    b, t, _ = u.shape
    rows = min(t, GMLP_CHUNK)
    n = t // rows
    mask = jnp.tril(jnp.ones((rows, rows), dtype=bool))
    ws = jnp.where(mask[None], w_s[:, :rows, :rows], 0.0).astype(gv.dtype)
    gvr = gv.reshape(b, n, rows, GMLP_GROUPS, GMLP_GROUP_DIM)
    mix = jnp.einsum('gts,bnsgc->bntgc', ws, gvr) + b_s[:, :rows].T[None, None, :, :, None]
    return u * mix.reshape(b, t, GMLP_WIDTH)


def layer_apply(x, c, attend, lambda_init, ada_w, ada_b, norm_g, ffn1_wgu, ffn1_wd, w_in,
                q_norm_g, k_norm_g, lambda_qk, attn_subln_g, gmlp_vnorm_g, gmlp_ws, gmlp_bs,
                w_gate, b_gate, w_branch, w_out, ffn2_wgu, ffn2_wd):
    b, t, _ = x.shape
    mods = (jax.nn.silu(c) @ ada_w + ada_b).reshape(b, N_MODS, D_MODEL)[:, :, None, :]
    sh1, sc1, gt1, sh2, sc2, gt2, sh3, sc3, gt3 = [mods[:, i] for i in range(N_MODS)]
    h = rms_norm(x, norm_g[0]) * (1 + sc1) + sh1
    x = x + 0.5 * gt1 * swiglu(h, ffn1_wgu, ffn1_wd)
    h = rms_norm(x, norm_g[1]) * (1 + sc2) + sh2
    z = h @ w_in
    q = rms_norm(z[..., :Q_WIDTH].reshape(b, t, N_HEADS_A, 2, HEAD_DIM_A), q_norm_g)
    k = rms_norm(z[..., Q_WIDTH:2 * Q_WIDTH].reshape(b, t, N_HEADS_A, 2, HEAD_DIM_A), k_norm_g)
    v = z[..., 2 * Q_WIDTH:2 * Q_WIDTH + ATTN_WIDTH].reshape(b, t, N_HEADS_A, V_DIM_A)
    lq = lambda_qk.astype(jnp.float32)
    lam = jnp.exp(jnp.sum(lq[0] * lq[1])) - jnp.exp(jnp.sum(lq[2] * lq[3])) + lambda_init
    o = attend(q, k, v, lam)
    o = (rms_norm(o, attn_subln_g) * (1.0 - lambda_init)).reshape(b, t, ATTN_WIDTH)
    gz = jax.nn.gelu(z[..., 2 * Q_WIDTH + ATTN_WIDTH:])
    u = gz[..., :GMLP_WIDTH]
    gv = rms_norm(gz[..., GMLP_WIDTH:], gmlp_vnorm_g)
    s = spatial_gate(u, gv, gmlp_ws, gmlp_bs)
    gates = jax.nn.sigmoid(h @ w_gate + b_gate)
    mixed = (gates[..., :D_MODEL] * (o @ w_branch[:ATTN_WIDTH])
             + gates[..., D_MODEL:] * (s @ w_branch[ATTN_WIDTH:]))
    x = x + gt2 * (mixed @ w_out)
    h = rms_norm(x, norm_g[2]) * (1 + sc3) + sh3
    x = x + 0.5 * gt3 * swiglu(h, ffn2_wgu, ffn2_wd)
    return rms_norm(x, norm_g[3]), k, v, gv


def setup_inputs(seed: int = 0) -> dict:
    key = jax.random.key(seed)
    ks = jax.random.split(key, 32)
    f32 = jnp.float32
    nrm = lambda k, shape, s: jax.random.normal(k, shape, f32) * s
    L = DEPTH
    return {
        "x_prompt": nrm(ks[0], (BATCH, SEQ, D_MODEL), 1.0),
        "x_sample": nrm(ks[1], (DEC_BATCH, DEC_SEQ, D_MODEL), 1.0),
        "cache_k": nrm(ks[2], (L, DEC_BATCH, PAST_LEN, N_HEADS_A, 2, HEAD_DIM_A), 1.0),
        "cache_v": nrm(ks[3], (L, DEC_BATCH, PAST_LEN, N_HEADS_A, V_DIM_A), 1.0),
        "c_prompt": nrm(ks[4], (BATCH, D_MODEL), 1.0),
        "c_sample": nrm(ks[5], (DEC_BATCH, D_MODEL), 1.0),
        "ada_w": nrm(ks[6], (L, D_MODEL, N_MODS * D_MODEL), 0.5 * D_MODEL ** -0.5),
        "ada_b": nrm(ks[7], (L, N_MODS * D_MODEL), 0.02),
        "norm_g": 1.0 + nrm(ks[8], (L, 4, D_MODEL), 0.02),
        "ffn1_wgu": nrm(ks[9], (L, D_MODEL, 2 * D_FF), D_MODEL ** -0.5),
        "ffn1_wd": nrm(ks[10], (L, D_FF, D_MODEL), D_FF ** -0.5),
        "w_in": nrm(ks[11], (L, D_MODEL, IN_WIDTH), D_MODEL ** -0.5),
        "q_norm_g": 1.0 + nrm(ks[12], (L, HEAD_DIM_A), 0.02),
        "k_norm_g": 1.0 + nrm(ks[13], (L, HEAD_DIM_A), 0.02),
        "lambda_qk": nrm(ks[14], (L, 4, HEAD_DIM_A), 0.1),
        "attn_subln_g": 1.0 + nrm(ks[15], (L, V_DIM_A), 0.02),
        "gmlp_vnorm_g": 1.0 + nrm(ks[16], (L, GMLP_WIDTH), 0.02),
        "gmlp_ws": nrm(ks[17], (L, GMLP_GROUPS, GMLP_CHUNK, GMLP_CHUNK), GMLP_CHUNK ** -0.5),
        "gmlp_bs": 1.0 + nrm(ks[18], (L, GMLP_GROUPS, GMLP_CHUNK), 0.02),
        "w_gate": nrm(ks[19], (L, D_MODEL, 2 * D_MODEL), D_MODEL ** -0.5),
        "b_gate": nrm(ks[20], (L, 2 * D_MODEL), 0.02),
        "w_branch": nrm(ks[21], (L, ATTN_WIDTH + GMLP_WIDTH, D_MODEL), ATTN_WIDTH ** -0.5),
        "w_out": nrm(ks[22], (L, D_MODEL, D_MODEL), D_MODEL ** -0.5),
        "ffn2_wgu": nrm(ks[23], (L, D_MODEL, 2 * D_FF), D_MODEL ** -0.5),
        "ffn2_wd": nrm(ks[24], (L, D_FF, D_MODEL), D_FF ** -0.5),
    }


def reference(x_prompt, x_sample, cache_k, cache_v, c_prompt, c_sample, ada_w, ada_b, norm_g,
              ffn1_wgu, ffn1_wd, w_in, q_norm_g, k_norm_g, lambda_qk, attn_subln_g, gmlp_vnorm_g,
              gmlp_ws, gmlp_bs, w_gate, b_gate, w_branch, w_out, ffn2_wgu, ffn2_wd):
    xp, xs = x_prompt, x_sample
    kp_list, vp_list, ks_list, vs_list, gs_list = [], [], [], [], []
    for l in range(DEPTH):
        w = (ada_w[l], ada_b[l], norm_g[l], ffn1_wgu[l], ffn1_wd[l], w_in[l], q_norm_g[l],
             k_norm_g[l], lambda_qk[l], attn_subln_g[l], gmlp_vnorm_g[l], gmlp_ws[l], gmlp_bs[l],
             w_gate[l], b_gate[l], w_branch[l], w_out[l], ffn2_wgu[l], ffn2_wd[l])
        lam0 = lambda_init_of(l)
        xp, kp, vp, _ = layer_apply(xp, c_prompt, attend_prompt, lam0, *w)
        xs, kn, vn, gvn = layer_apply(
            xs, c_sample, functools.partial(attend_sample, ck=cache_k[l], cv=cache_v[l]), lam0, *w)
        kp_list.append(kp)
        vp_list.append(vp)
        ks_list.append(kn)
        vs_list.append(vn)
        gs_list.append(gvn)
    k_prompt = jnp.stack(kp_list)
    v_prompt = jnp.stack(vp_list)
    k_sample = jnp.stack(ks_list)
    v_sample = jnp.stack(vs_list)
    gmlp_v_sample = jnp.stack(gs_list)
    return (xp, xs, k_prompt, v_prompt, k_sample, v_sample, gmlp_v_sample)
```

```python
import contextlib
import numpy as np
import concourse.bass as bass
import concourse.mybir as mybir
from concourse.bass_utils import run_bass_kernel_spmd

F32 = mybir.dt.float32
BF16 = mybir.dt.bfloat16
AF = mybir.ActivationFunctionType
ALU = mybir.AluOpType
AX = mybir.AxisListType

D = 1024
SEQ = 8192
DS = 16
PAST = 1024
H = 8
DFF = 2816
NJ = 22
LAM0 = 0.2
EPS = 1e-6
NDIST = 68
ALIBI_SKIP = 168.0
N_PROMPT_TILES = SEQ // 512

ENGS = ("pe", "act", "dve", "pool", "sp")


class Res:
    __slots__ = ("w", "r")

    def __init__(self):
        self.w = None
        self.r = []


class Op:
    __slots__ = ("eng", "fn", "deps", "sig", "dma", "sem", "val")

    def __init__(self, eng, fn, dma):
        self.eng = eng
        self.fn = fn
        self.dma = dma
        self.deps = []
        self.sig = False
        self.sem = None
        self.val = 0


class Sched:
    def __init__(self, nc, n_dma_sems=12, same_engine_waits=True):
        self.nc = nc
        self.ops = {e: [] for e in ENGS}
        self.n_dma_sems = n_dma_sems
        self.same_engine_waits = same_engine_waits

    def add(self, eng, fn, reads=(), writes=(), dma=False):
        op = Op(eng, fn, dma)
        deps = []
        for r in reads:
            if r.w is not None:
                deps.append(r.w)
        for w in writes:
            if w.w is not None:
                deps.append(w.w)
            deps.extend(w.r)
        seen = set()
        for d in deps:
            if id(d) not in seen:
                seen.add(id(d))
                op.deps.append(d)
        for r in reads:
            if not dma:
                r.r = [x for x in r.r if x.dma or x.eng != eng]
            r.r.append(op)
        for w in writes:
            w.w = op
            w.r = []
        self.ops[eng].append(op)
        return op

    def pe(self, fn, reads=(), writes=()):
        return self.add("pe", fn, reads, writes)

    def act(self, fn, reads=(), writes=()):
        return self.add("act", fn, reads, writes)

    def dve(self, fn, reads=(), writes=()):
        return self.add("dve", fn, reads, writes)

    def dma(self, q, fn, reads=(), writes=()):
        return self.add(q, fn, reads, writes, dma=True)

    def emit(self, final_wait_ops=()):
        nc = self.nc
        sew = self.same_engine_waits
        for e in ENGS:
            for op in self.ops[e]:
                for d in op.deps:
                    if d.dma or op.dma or d.eng != op.eng or (sew and op.eng != "pe"):
                        d.sig = True
        for op in final_wait_ops:
            op.sig = True
        with contextlib.ExitStack() as st:
            esem = {e: st.enter_context(nc.semaphore("s_" + e)) for e in ENGS}
            dsem = {}
            for q in ("sp", "pool"):
                dsem[q] = [st.enter_context(nc.semaphore("d_%s%d" % (q, i)))
                           for i in range(self.n_dma_sems)]
            for e in ENGS:
                c = 0
                di = 0
                dcnt = [0] * self.n_dma_sems
                prev = [None] * self.n_dma_sems
                for op in self.ops[e]:
                    if op.dma:
                        s = di % self.n_dma_sems
                        di += 1
                        dcnt[s] += 16
                        op.sem = dsem[e][s]
                        op.val = dcnt[s]
                        if prev[s] is not None:
                            op.deps.append(prev[s])
                        prev[s] = op
                        op.sig = True
                    elif op.sig:
                        c += 1
                        op.sem = esem[e]
                        op.val = c
            block = st.enter_context(nc.Block())

            def run_engine(e, eng):
                seen = {}
                for op in self.ops[e]:
                    for d in op.deps:
                        if d.sem is None:
                            continue
                        if (not d.dma) and (not op.dma) and d.eng == e:
                            if e == "pe" or not sew:
                                continue
                        key = id(d.sem)
                        if seen.get(key, 0) >= d.val:
                            continue
                        seen[key] = d.val
                        eng.wait_ge(d.sem, d.val)
                    ins = op.fn(eng)
                    if op.sig:
                        ins.then_inc(op.sem, 16 if op.dma else 1)
                if e == "sp":
                    for op in final_wait_ops:
                        eng.wait_ge(op.sem, op.val)

            @block.tensor
            def _(eng):
                run_engine("pe", eng)

            @block.scalar
            def _(eng):
                run_engine("act", eng)

            @block.vector
            def _(eng):
                run_engine("dve", eng)

            @block.gpsimd
            def _(eng):
                run_engine("pool", eng)

            @block.sync
            def _(eng):
                run_engine("sp", eng)


class TB:
    def __init__(self, t, n=1):
        self.t = t
        self.r = [Res() for _ in range(n)]


class Builder:
    def __init__(self, n_prompt_tiles=N_PROMPT_TILES, do_sample=True, debug=False, debug_kind="sample", stage=99):
        self.stage = int(stage)
        self.substage = int(round((stage - int(stage)) * 10)) if stage != int(stage) else 99
        self.debug = debug
        self.debug_kind = debug_kind
        self.n_prompt_tiles = n_prompt_tiles
        self.do_sample = do_sample
        self.nc = bass.Bass("TRN2", target_bir_lowering=False)
        self.st = contextlib.ExitStack()
        self.S = Sched(self.nc)
        self.fin = []
        self._scr_i = 0
        self._pb_i = 0
        self._reserved = set()
        self._wblocks = []
        self._w_issued = 0
        self._w_cursor = 0
        self._rr = 0

    def din(self, name, shape, dt=F32):
        return self.nc.dram_tensor(name, list(shape), dt, kind="ExternalInput").ap()

    def dout(self, name, shape, dt=F32):
        return self.nc.dram_tensor(name, list(shape), dt, kind="ExternalOutput").ap()

    def dscr(self, name, shape, dt):
        return self.nc.dram_tensor(name, list(shape), dt).ap()

    def sb(self, name, shape, dt, n=1):
        return TB(self.st.enter_context(self.nc.sbuf_tensor(name, list(shape), dt)), n)

    def ps(self, name, shape, dt):
        return TB(self.st.enter_context(self.nc.psum_tensor(name, list(shape), dt)), 1)

    def scr(self):
        b = self.scrs[self._scr_i % len(self.scrs)]
        self._scr_i += 1
        return b

    def bank(self):
        while (self._pb_i % 8) in self._reserved:
            self._pb_i += 1
        b = self.pb[self._pb_i % 8]
        self._pb_i += 1
        return b

    def alt(self):
        self._rr += 1
        return self._rr % 2

    def declare(self):
        nc = self.nc
        self.xp = self.din("xp", [SEQ, D])
        self.xs = self.din("xs", [DS, D])
        self.ck = self.din("ck", [PAST, D])
        self.cv = self.din("cv", [PAST, D])
        self.cvecT = self.din("cvecT", [128, 16])
        self.ada_w = self.din("ada_w", [D, 9 * D])
        self.ada_bT = self.din("ada_bT", [128, 72])
        self.ngT = self.din("ngT", [128, 32])
        self.w1gu = self.din("w1gu", [D, 2 * DFF])
        self.w1d = self.din("w1d", [DFF, D])
        self.w_in = self.din("w_in", [D, 5 * D])
        self.gq_b = self.din("gq_b", [128, 64])
        self.gk_b = self.din("gk_b", [128, 64])
        self.lq_b = self.din("lq_b", [128, 256])
        self.sublnT = self.din("sublnT", [128, 1])
        self.vng_b = self.din("vng_b", [128, D])
        self.ws = self.din("ws", [8, 128, 128])
        self.bs_b = self.din("bs_b", [128, D])
        self.w_gate = self.din("w_gate", [D, 2 * D])
        self.bgT = self.din("bgT", [128, 16])
        self.w_br = self.din("w_br", [2 * D, D])
        self.w_out = self.din("w_out", [D, D])
        self.w2gu = self.din("w2gu", [D, 2 * DFF])
        self.w2d = self.din("w2d", [DFF, D])
        self.c_ident = self.din("c_ident", [128, 128])
        self.c_biasH = self.din("c_biasH", [128, H * NDIST])
        self.c_cmask = self.din("c_cmask", [128, H * 128])
        self.c_qaug = self.din("c_qaug", [2, 16 * 512])
        self.c_tril = self.din("c_tril", [128, 128])

        self.yp = self.dout("yp", [SEQ, D])
        self.ys = self.dout("ys", [DS, D])
        self.kp = self.dout("kp", [SEQ, D])
        self.vp = self.dout("vp", [SEQ, D])
        self.ks = self.dout("ks", [DS, D])
        self.vs = self.dout("vs", [DS, D])
        self.gs = self.dout("gs", [DS, D])

        self.ktscr_p = self.dscr("ktscr_p", [H, 2, 64, SEQ], BF16)
        self.vscr_p = self.dscr("vscr_p", [128, SEQ // 128, D], BF16)
        self.ktscr_s = self.dscr("ktscr_s", [H, 2, 64, 1536], BF16)
        self.vscr_s = self.dscr("vscr_s", [128, 12, D], BF16)
        self.r_scr_p = [(Res(), Res()) for _ in range(SEQ // 512)]
        self.r_scr_s = [(Res(), Res()) for _ in range(3)]

        sb = self.sb
        self.ident_f = sb("ident_f", [128, 128], F32)
        self.ident_b = sb("ident_b", [128, 128], BF16)
        self.ones_b = sb("ones_b", [128, 128], BF16)
        self.epsT = sb("epsT", [128, 1], F32)
        self.biasH = sb("biasH", [128, H * NDIST], F32)
        self.cmask = sb("cmask", [128, H, 128], BF16)
        self.trilT = sb("trilT", [128, 128], F32)
        self.wsf = None
        self.wsT = sb("wsT", [128, 8, 128], BF16)
        self.bsb = sb("bsb", [128, 8, 128], F32)
        self.vngb = sb("vngb", [128, D], F32)
        self.gq8 = sb("gq8", [128, 64], F32)
        self.gkb = sb("gkb", [128, 64], F32)
        self.lqb = sb("lqb", [128, 256], F32)
        self.lam_t = sb("lam_t", [128, 8], F32)
        self.neg_lam = sb("neg_lam", [128, 1], F32)
        self.ng = sb("ng", [128, 4, 8], F32)
        self.bgate = sb("bgate", [128, 16], F32)
        self.subgs = sb("subgs", [128, 1], F32)
        self.adab = sb("adab", [128, 72], F32)
        self.cvt = sb("cvt", [128, 16], F32)
        self.cs = sb("cs", [128, 16], F32)
        self.modT = sb("modT", [128, 72, 2], F32)
        self.dv = sb("dv", [128, 2, 10, 8], F32)
        self.st8 = [sb("st8_%d" % i, [128, 8], F32) for i in range(4)]
        self._st8_i = 0

        self.xin = [sb("xin%d" % i, [128, D], F32) for i in range(2)]
        self.yout = [sb("yout%d" % i, [128, D], F32) for i in range(2)]
        self.xT = sb("xT", [128, 8, 512], F32, n=8)
        self.sqc = [sb("sqc%d" % i, [128, 512], BF16) for i in range(2)]
        self.hT = sb("hT", [128, 8, 512], BF16, n=8)
        self.big = sb("big", [128, 32, 512], BF16, n=32)
        self.scrs = [sb("scr%d" % i, [128, 512], F32) for i in range(6)]
        self.rstdb = [sb("rstdb%d" % i, [128, 512], F32) for i in range(2)]
        self.knf = [sb("knf%d" % i, [128, 512], F32) for i in range(2)]
        self.knb = [sb("knb%d" % i, [128, 512], BF16) for i in range(3)]
        self.qnb = [sb("qnb%d" % i, [128, 512], BF16) for i in range(3)]
        self.vf = [sb("vf%d" % i, [128, 512], F32) for i in range(2)]
        self.vst = [sb("vst%d" % i, [128, 512], BF16) for i in range(2)]
        self.gzf = sb("gzf", [128, D], F32)
        self.QT = sb("QT", [66, 16, 512], BF16, n=16)
        self.r_qaug = Res()
        self._r_od = self.gzf.r * 2
        self.KTst = [sb("KTst%d" % i, [64, 8, 128], BF16) for i in range(2)]
        self.slotK = [sb("slotK%d" % i, [66, 2, 512], BF16) for i in range(4)]
        self.slotV = [sb("slotV%d" % i, [128, 4, 128], BF16) for i in range(4)]
        self.r_slotK = [Res() for _ in range(4)]
        self.r_slotV = [Res() for _ in range(4)]
        self.r_slot_ones = [Res() for _ in range(4)]
        self.PT = [sb("PT%d" % i, [128, 2, 512], BF16) for i in range(2)]
        self._pt_i = 0
        self.osq = [sb("osq%d" % i, [128, 512], BF16) for i in range(2)]
        self.oT = sb("oT", [128, 8, 512], BF16, n=8)
        self.wsl = [sb("wsl%d" % i, [128, 4096], BF16) for i in range(4)]

        self.pb = [self.ps("pb%d" % i, [128, 512], F32) for i in range(4)]
        self.SP = [self.ps("sp%d" % i, [128, 2, 512], F32) for i in range(2)]
        for i in range(2):
            for m in range(2):
                hb = TB(self.SP[i].t[:, m, :], 1)
                self.pb.append(hb)
            self.SP[i].r = [self.pb[4 + 2 * i].r[0], self.pb[5 + 2 * i].r[0]]
        self._cnt = {k: 0 for k in ("knf", "knb", "qnb", "vf", "vst", "KTst", "xin", "yout", "sqc", "rstdb")}
        print("SBUF bytes remaining per partition:", self.nc.sbuf_bytes_remaining)

    def dump(self, name, ap, shape, reads, dt=F32):
        if not getattr(self, "debug", False):
            return
        d = self.nc.dram_tensor("dbg_" + name, list(shape), dt, kind="ExternalOutput").ap()
        self.fin.append(self.S.dma("sp", lambda e: e.dma_start(out=d, in_=ap), reads=reads))

    def rot(self, name):
        lst = getattr(self, name)
        i = self._cnt[name]
        self._cnt[name] = i + 1
        return lst[i % len(lst)]

    def st8n(self):
        b = self.st8[self._st8_i % 4]
        self._st8_i += 1
        return b

    def plan_weights(self, ntiles_total):
        def blk_cols(w, kc0, nkc, col0, ncols):
            return w[kc0 * 128:(kc0 + nkc) * 128, col0:col0 + ncols].rearrange("(kc p) n -> p kc n", p=128)

        def ffn_blocks(wgu, wd):
            out = []
            for p in range(NJ // 2):
                out.append(("gu", [(0, 256, blk_cols(wgu, 0, 8, 256 * p, 256)),
                                   (256, 256, blk_cols(wgu, 0, 8, DFF + 256 * p, 256))], 8, 512))
            for cp in range(4):
                for half in range(2):
                    out.append(("d", [(0, 256, blk_cols(wd, half * 11, 11, cp * 256, 256))], 11, 256))
            return out

        tmpl = ffn_blocks(self.w1gu, self.w1d)
        for cg in range(6):
            tmpl.append(("in", [(0, 512, blk_cols(self.w_in, 0, 8, cg * 512, 512))], 8, 512))
        for cg in range(2):
            tmpl.append(("gv", [(0, 512, blk_cols(self.w_in, 0, 8, 4096 + cg * 512, 512))], 8, 512))
        for cg in range(2):
            tmpl.append(("u", [(0, 512, blk_cols(self.w_in, 0, 8, 3072 + cg * 512, 512))], 8, 512))
        for cb in range(2):
            tmpl.append(("brA", [(0, 512, blk_cols(self.w_br, 0, 8, cb * 512, 512))], 8, 512))
            tmpl.append(("brB", [(0, 512, blk_cols(self.w_br, 8, 8, cb * 512, 512))], 8, 512))
            tmpl.append(("gA", [(0, 512, blk_cols(self.w_gate, 0, 8, cb * 512, 512))], 8, 512))
            tmpl.append(("gB", [(0, 512, blk_cols(self.w_gate, 0, 8, D + cb * 512, 512))], 8, 512))
        for cb in range(2):
            tmpl.append(("out", [(0, 512, blk_cols(self.w_out, 0, 8, cb * 512, 512))], 8, 512))
        tmpl += ffn_blocks(self.w2gu, self.w2d)
        self._wtmpl = tmpl
        self._wn = len(tmpl)
        self._wtotal = self._wn * ntiles_total
        self._w_res = [[Res(), Res()] for _ in range(4)]
        self.wscr = self.dscr("wscr", [self._wn, 128, 4096], BF16)
        self._r_wscr = [Res() for _ in range(self._wn)]

    def _issue_w(self):
        i = self._w_issued
        b = i % self._wn
        ti = i // self._wn
        kind, parts, nkc, ncols = self._wtmpl[b]
        slot = self.wsl[i % 4]
        res = self._w_res[i % 4]
        view = slot.t[:, 0:nkc * ncols].rearrange("p (kc n) -> p kc n", n=ncols)
        if ti >= 2:
            self.S.dma("pool", lambda e, o=slot.t[:, 0:nkc * ncols], s_=self.wscr[b, :, 0:nkc * ncols]: e.dma_start(out=o, in_=s_),
                       writes=[res[0], self._r_wscr[b]])
        else:
            for pi, (c0, nc_, src) in enumerate(parts):
                self.S.dma("pool", lambda e, o=view[:, :, c0:c0 + nc_], s_=src: e.dma_start(out=o, in_=s_),
                           writes=[res[pi]])
            if ti == 0 and self._wtotal > 2 * self._wn:
                dview = self.wscr[b, :, 0:nkc * ncols].rearrange("p (kc n) -> p kc n", n=ncols)
                for pi, (c0, nc_, src) in enumerate(parts):
                    self.S.dma("pool", lambda e, o=dview[:, :, c0:c0 + nc_], s_=src: e.dma_start(out=o, in_=s_),
                               reads=[self._r_wscr[b]])
        self._w_issued += 1

    def wgroup(self, kinds):
        i0 = self._w_cursor
        out = []
        lim = min(self._wtotal, i0 + 4)
        assert len(kinds) <= 4
        while self._w_issued < lim:
            self._issue_w()
        for n, kind in enumerate(kinds):
            i = i0 + n
            k, parts, nkc, ncols = self._wtmpl[i % self._wn]
            assert k == kind, (k, kind, i)
            slot = self.wsl[i % 4]
            view = slot.t[:, 0:nkc * ncols].rearrange("p (kc n) -> p kc n", n=ncols)
            out.append((view, self._w_res[i % 4]))
        self._w_cursor += len(kinds)
        return out

    def wnext(self, kind):
        return self.wgroup([kind])[0]

    def prologue(self):
        S = self.S

        def ld(q, dst, src, reads=()):
            return S.dma(q, lambda e, o=dst.t[:], s=src: e.dma_start(out=o, in_=s), reads=reads, writes=dst.r)

        ld("sp", self.ident_f, self.c_ident)
        ld("pool", self.ident_b, self.c_ident)
        ld("sp", self.biasH, self.c_biasH)
        S.dma("pool", lambda e: e.dma_start(out=self.cmask.t[:].rearrange("p h n -> p (h n)"), in_=self.c_cmask),
              writes=self.cmask.r)
        ld("sp", self.trilT, self.c_tril)
        S.dma("sp", lambda e: e.dma_start(out=self.bsb.t[:].rearrange("p g n -> p (g n)"), in_=self.bs_b),
              writes=self.bsb.r)
        ld("sp", self.vngb, self.vng_b)
        ld("sp", self.gq8, self.gq_b)
        ld("sp", self.gkb, self.gk_b)
        ld("sp", self.lqb, self.lq_b)
        S.dma("sp", lambda e: e.dma_start(out=self.ng.t[:].rearrange("p i c -> p (i c)"), in_=self.ngT),
              writes=self.ng.r)
        ld("sp", self.bgate, self.bgT)
        ld("sp", self.subgs, self.sublnT)
        ld("sp", self.adab, self.ada_bT)
        ld("sp", self.cvt, self.cvecT)
        S.dma("pool", lambda e: e.dma_start(out=self.QT.t[64:66, :, :].rearrange("p a n -> p (a n)"), in_=self.c_qaug),
              writes=[self.r_qaug])
        S.dve(lambda e: e.memset(self.ones_b.t[:], 1.0), writes=self.ones_b.r)
        S.dve(lambda e: e.memset(self.epsT.t[:], EPS), writes=self.epsT.r)
        for i in range(4):
            S.dve(lambda e, i=i: e.memset(self.slotK[i].t[64:66, :, :], 1.0), writes=[self.r_slot_ones[i]])
        S.dve(lambda e: e.tensor_single_scalar(out=self.gq8.t[:], in_=self.gq8.t[:], scalar=0.125, op=ALU.mult),
              reads=self.gq8.r, writes=self.gq8.r)
        S.dve(lambda e: e.tensor_single_scalar(out=self.subgs.t[:], in_=self.subgs.t[:], scalar=1.0 - LAM0, op=ALU.mult),
              reads=self.subgs.r, writes=self.subgs.r)
        lt = self.lam_t
        tmp = self.scr()
        S.dve(lambda e: e.tensor_tensor(out=tmp.t[:, 0:64], in0=self.lqb.t[:, 0:64], in1=self.lqb.t[:, 64:128], op=ALU.mult),
              reads=self.lqb.r, writes=tmp.r)
        S.dve(lambda e: e.tensor_tensor(out=tmp.t[:, 64:128], in0=self.lqb.t[:, 128:192], in1=self.lqb.t[:, 192:256],
                                        op=ALU.mult), reads=self.lqb.r + tmp.r, writes=tmp.r)
        S.dve(lambda e: e.tensor_reduce(out=lt.t[:, 0:2], in_=tmp.t[:, 0:128].rearrange("p (a d) -> p a d", d=64),
                                        axis=AX.X, op=ALU.add), reads=tmp.r, writes=lt.r)
        S.act(lambda e: e.activation(out=lt.t[:, 2:4], in_=lt.t[:, 0:2], func=AF.Exp), reads=lt.r, writes=lt.r)
        S.dve(lambda e: e.tensor_tensor(out=lt.t[:, 4:5], in0=lt.t[:, 3:4], in1=lt.t[:, 2:3], op=ALU.subtract),
              reads=lt.r, writes=lt.r)
        S.dve(lambda e: e.tensor_single_scalar(out=self.neg_lam.t[:], in_=lt.t[:, 4:5], scalar=-LAM0, op=ALU.add),
              reads=lt.r, writes=self.neg_lam.r)
        wsf_view = self.big.t[:, 0:4, :].rearrange("p a n -> p (a n)").bitcast(F32)
        wsf = wsf_view.rearrange("p (g s) -> p g s", s=128)
        rbig = self.big.r[0:4]
        S.dma("sp", lambda e: e.dma_start(out=wsf, in_=self.ws.rearrange("g t s -> t g s")), writes=rbig)
        for half in range(2):
            bk = self.bank()
            for gg in range(4):
                g = half * 4 + gg
                S.pe(lambda e, g=g, gg=gg, bk=bk: e.transpose(out=bk.t[:, gg * 128:(gg + 1) * 128], in_=wsf[:, g, :],
                                                              identity=self.ident_f.t[:]),
                     reads=rbig + self.ident_f.r, writes=bk.r)
            S.dve(lambda e, half=half, bk=bk: e.tensor_tensor(
                out=self.wsT.t[:, half * 4:(half + 1) * 4, :],
                in0=bk.t[:].rearrange("p (g t) -> p g t", t=128),
                in1=self.trilT.t[:].rearrange("p (o t) -> p o t", o=1).to_broadcast([128, 4, 128]),
                op=ALU.mult), reads=bk.r + self.trilT.r, writes=self.wsT.r)
        S.act(lambda e: e.activation(out=self.cs.t[:], in_=self.cvt.t[:], func=AF.Silu), reads=self.cvt.r, writes=self.cs.r)
        mbank = self.pb[7]
        self._reserved = {7}
        rhalf = [self.xT.r[0:4], self.xT.r[4:8]]
        nblk = 9 * D // 256
        for b in range(nblk):
            hb = b % 2
            slot = self.xT.t[:, hb * 4:(hb + 1) * 4, :].rearrange("p a (b n) -> p (a b) n", n=256)
            src = self.ada_w[:, b * 256:(b + 1) * 256].rearrange("(kc p) n -> p kc n", p=128)
            S.dma("sp", lambda e, o=slot, s=src: e.dma_start(out=o, in_=s), writes=rhalf[hb])
            bk = self.bank()
            for c in range(8):
                S.pe(lambda e, c=c, bk=bk, slot=slot: e.matmul(bk.t[0:2, 0:256], lhsT=self.cs.t[:, 2 * c:2 * c + 2],
                                                               rhs=slot[:, c, :], start=(c == 0), stop=(c == 7)),
                     reads=self.cs.r + rhalf[hb], writes=bk.r)
            mrow = self.scr()
            S.act(lambda e, bk=bk, mrow=mrow: e.copy(out=mrow.t[0:2, 0:256], in_=bk.t[0:2, 0:256]), reads=bk.r, writes=mrow.r)
            for k in range(2):
                j = 2 * b + k
                S.pe(lambda e, k=k, j=j, mrow=mrow: e.transpose(out=mbank.t[:, 2 * j:2 * j + 2],
                                                                in_=mrow.t[0:2, k * 128:(k + 1) * 128],
                                                                identity=self.ident_f.t[0:2, 0:2]),
                     reads=mrow.r + self.ident_f.r, writes=mbank.r)
        S.dve(lambda e: e.tensor_tensor(out=self.modT.t[:], in0=mbank.t[:, 0:144].rearrange("p (j s) -> p j s", s=2),
                                        in1=self.adab.t[:].rearrange("p (j o) -> p j o", o=1).to_broadcast([128, 72, 2]),
                                        op=ALU.add), reads=mbank.r + self.adab.r, writes=self.modT.r)
        self._reserved = set()
        for s in range(2):
            for n in range(3):
                sc = self.modT.t[:, (3 * n + 1) * 8:(3 * n + 2) * 8, s]
                sh = self.modT.t[:, (3 * n) * 8:(3 * n + 1) * 8, s]
                gt = self.modT.t[:, (3 * n + 2) * 8:(3 * n + 3) * 8, s]
                S.dve(lambda e, s=s, n=n, sc=sc: e.scalar_tensor_tensor(out=self.dv.t[:, s, n, :], in0=sc, scalar=1.0,
                                                                        in1=self.ng.t[:, n, :], op0=ALU.add, op1=ALU.mult),
                      reads=self.modT.r + self.ng.r, writes=self.dv.r)
                S.dve(lambda e, s=s, n=n, sh=sh: e.tensor_copy(out=self.dv.t[:, s, 3 + n, :], in_=sh),
                      reads=self.modT.r, writes=self.dv.r)
                S.dve(lambda e, s=s, n=n, gt=gt: e.tensor_single_scalar(out=self.dv.t[:, s, 6 + n, :], in_=gt,
                                                                        scalar=(1.0 if n == 1 else 0.5), op=ALU.mult),
                      reads=self.modT.r, writes=self.dv.r)
            S.dve(lambda e, s=s: e.tensor_copy(out=self.dv.t[:, s, 9, :], in_=self.ng.t[:, 3, :]),
                  reads=self.ng.r, writes=self.dv.r)

    def dvs(self, s, k, c):
        return self.dv.t[:, s, k, c:c + 1]

    def rms_stats(self, NT, src_fn, src_res_fn, nchunks, inv_n):
        S = self.S
        bk = self.bank()
        for c in range(nchunks):
            sq = self.rot("sqc")
            S.act(lambda e, c=c, sq=sq: e.activation(out=sq.t[:, 0:NT], in_=src_fn(c), func=AF.Square),
                  reads=src_res_fn(c), writes=sq.r)
            S.pe(lambda e, c=c, sq=sq: e.matmul(bk.t[:, 0:NT], lhsT=self.ones_b.t[:], rhs=sq.t[:, 0:NT],
                                                start=(c == 0), stop=(c == nchunks - 1)),
                 reads=sq.r + self.ones_b.r, writes=bk.r)
        rt = self.scr()
        S.act(lambda e: e.activation(out=rt.t[:, 0:NT], in_=bk.t[:, 0:NT], func=AF.Ln, bias=self.epsT.t[:], scale=inv_n),
              reads=bk.r + self.epsT.r, writes=rt.r)
        rstd = self.rot("rstdb")
        S.act(lambda e: e.activation(out=rstd.t[:, 0:NT], in_=rt.t[:, 0:NT], func=AF.Exp, scale=-0.5), reads=rt.r, writes=rstd.r)
        return rstd

    def rmsnorm_mod(self, NT, s, n):
        S = self.S
        rstd = self.rms_stats(NT, lambda c: self.xT.t[:, c, 0:NT], lambda c: [self.xT.r[c]], 8, 1.0 / D)
        for c in range(8):
            t = self.scr()
            S.dve(lambda e, c=c, t=t: e.tensor_tensor(out=t.t[:, 0:NT], in0=self.xT.t[:, c, 0:NT], in1=rstd.t[:, 0:NT],
                                                      op=ALU.mult), reads=[self.xT.r[c]] + rstd.r, writes=t.r)
            S.act(lambda e, c=c, t=t: e.activation(out=self.hT.t[:, c, 0:NT], in_=t.t[:, 0:NT], func=AF.Identity,
                                                   bias=self.dvs(s, 3 + n, c), scale=self.dvs(s, n, c)),
                  reads=t.r + self.dv.r, writes=[self.hT.r[c]])

    def ffn(self, NT, s, gt_k):
        S = self.S
        hid = self.big
        for p in range(NJ // 2):
            w, wr = self.wnext("gu")
            for jj in range(2):
                j = 2 * p + jj
                bg = self.bank()
                for c in range(8):
                    S.pe(lambda e, c=c, bg=bg, w=w, jj=jj: e.matmul(bg.t[:, 0:NT], lhsT=w[:, c, jj * 128:(jj + 1) * 128],
                                                                     rhs=self.hT.t[:, c, 0:NT], start=(c == 0), stop=(c == 7)),
                         reads=wr + [self.hT.r[c]], writes=bg.r)
                bu = self.bank()
                for c in range(8):
                    S.pe(lambda e, c=c, bu=bu, w=w, jj=jj: e.matmul(bu.t[:, 0:NT],
                                                                     lhsT=w[:, c, 256 + jj * 128:256 + (jj + 1) * 128],
                                                                     rhs=self.hT.t[:, c, 0:NT], start=(c == 0), stop=(c == 7)),
                         reads=wr + [self.hT.r[c]], writes=bu.r)
                sg = self.scr()
                S.act(lambda e, bg=bg, sg=sg: e.activation(out=sg.t[:, 0:NT], in_=bg.t[:, 0:NT], func=AF.Silu),
                      reads=bg.r, writes=sg.r)
                S.dve(lambda e, bu=bu, sg=sg, j=j: e.tensor_tensor(out=hid.t[:, j, 0:NT], in0=bu.t[:, 0:NT], in1=sg.t[:, 0:NT],
                                                                   op=ALU.mult), reads=bu.r + sg.r, writes=[hid.r[j]])
        for cp in range(4):
            (w0, wr0), (w1, wr1) = self.wgroup(["d", "d"])
            for mm in range(2):
                mc = 2 * cp + mm
                bk = self.bank()
                for kc in range(NJ):
                    w, wr = (w0, wr0) if kc < 11 else (w1, wr1)
                    S.pe(lambda e, kc=kc, bk=bk, w=w, mm=mm: e.matmul(bk.t[:, 0:NT], lhsT=w[:, kc % 11, mm * 128:(mm + 1) * 128],
                                                                       rhs=hid.t[:, kc, 0:NT], start=(kc == 0),
                                                                       stop=(kc == NJ - 1)),
                         reads=wr + [hid.r[kc]], writes=bk.r)
                S.dve(lambda e, bk=bk, mc=mc: e.scalar_tensor_tensor(out=self.xT.t[:, mc, 0:NT], in0=bk.t[:, 0:NT],
                                                                     scalar=self.dvs(s, gt_k, mc), in1=self.xT.t[:, mc, 0:NT],
                                                                     op0=ALU.mult, op1=ALU.add),
                      reads=bk.r + self.dv.r + [self.xT.r[mc]], writes=[self.xT.r[mc]])

    def prefetch_x(self, x_dram, row0, TS, nsub):
        S = self.S
        lst = []
        for i in range(min(2, nsub)):
            xi = self.rot("xin")
            S.dma("sp", lambda e, xi=xi, i=i: e.dma_start(out=xi.t[0:TS, :], in_=x_dram[row0 + i * TS:row0 + (i + 1) * TS, :]),
                  writes=xi.r)
            lst.append(xi)
        self._x_pref = (row0, id(x_dram), lst)

    def load_x(self, x_dram, row0, NT, TS, nsub):
        S = self.S
        pref = getattr(self, "_x_pref", None)
        if pref is not None and (pref[0], pref[1]) != (row0, id(x_dram)):
            pref = None
        self._x_pref = None
        for g0 in range(0, nsub, 2):
            subs = list(range(g0, min(nsub, g0 + 2)))
            xs_ = []
            for n, i in enumerate(subs):
                if g0 == 0 and pref is not None:
                    xs_.append(pref[2][n])
                    continue
                xi = self.rot("xin")
                S.dma("sp", lambda e, xi=xi, i=i: e.dma_start(out=xi.t[0:TS, :], in_=x_dram[row0 + i * TS:row0 + (i + 1) * TS, :]),
                      writes=xi.r)
                xs_.append(xi)
            for c in range(8):
                bk = self.bank()
                for n, i in enumerate(subs):
                    S.pe(lambda e, c=c, n=n, bk=bk, xi=xs_[n]: e.transpose(out=bk.t[:, n * TS:(n + 1) * TS],
                                                                          in_=xi.t[0:TS, c * 128:(c + 1) * 128],
                                                                          identity=self.ident_f.t[0:TS, 0:TS]),
                         reads=xs_[n].r + self.ident_f.r, writes=bk.r)
                w = len(subs) * TS
                if self.alt():
                    S.act(lambda e, c=c, bk=bk, w=w, g0=g0: e.copy(out=self.xT.t[:, c, g0 * TS:g0 * TS + w], in_=bk.t[:, 0:w]),
                          reads=bk.r, writes=[self.xT.r[c]])
                else:
                    S.dve(lambda e, c=c, bk=bk, w=w, g0=g0: e.tensor_copy(out=self.xT.t[:, c, g0 * TS:g0 * TS + w], in_=bk.t[:, 0:w]),
                          reads=bk.r, writes=[self.xT.r[c]])

    def store_y(self, y_dram, row0, NT, TS, nsub, s):
        S = self.S
        rstd = self.rms_stats(NT, lambda c: self.xT.t[:, c, 0:NT], lambda c: [self.xT.r[c]], 8, 1.0 / D)
        for c in range(8):
            S.dve(lambda e, c=c: e.scalar_tensor_tensor(out=self.xT.t[:, c, 0:NT], in0=self.xT.t[:, c, 0:NT],
                                                        scalar=self.dvs(s, 9, c), in1=rstd.t[:, 0:NT],
                                                        op0=ALU.mult, op1=ALU.mult),
                  reads=[self.xT.r[c]] + rstd.r + self.dv.r, writes=[self.xT.r[c]])
        for i in range(nsub):
            yo = self.rot("yout")
            for cq in range(2):
                bk = self.bank()
                for k in range(4):
                    c = cq * 4 + k
                    S.pe(lambda e, c=c, k=k, bk=bk, i=i: e.transpose(out=bk.t[0:TS, k * 128:(k + 1) * 128],
                                                                     in_=self.xT.t[:, c, i * TS:(i + 1) * TS],
                                                                     identity=self.ident_f.t[:]),
                         reads=[self.xT.r[c]] + self.ident_f.r, writes=bk.r)
                if self.alt():
                    S.act(lambda e, bk=bk, yo=yo, cq=cq: e.copy(out=yo.t[0:TS, cq * 512:(cq + 1) * 512], in_=bk.t[0:TS, :]),
                          reads=bk.r, writes=yo.r)
                else:
                    S.dve(lambda e, bk=bk, yo=yo, cq=cq: e.tensor_copy(out=yo.t[0:TS, cq * 512:(cq + 1) * 512], in_=bk.t[0:TS, :]),
                          reads=bk.r, writes=yo.r)
            self.fin.append(S.dma("sp", lambda e, yo=yo, i=i: e.dma_start(out=y_dram[row0 + i * TS:row0 + (i + 1) * TS, :],
                                                                          in_=yo.t[0:TS, :]), reads=yo.r))

    def kT_store(self, src_b, TS, hm0, ktscr, key0, rscr):
        S = self.S
        bk = self.bank()
        bkb = bk.t[:].bitcast(BF16)
        for a in range(8):
            S.pe(lambda e, a=a: e.transpose(out=bkb[0:64, a * 128:a * 128 + TS], in_=src_b.t[0:TS, a * 64:(a + 1) * 64],
                                            identity=self.ident_b.t[0:TS, 0:TS]),
                 reads=src_b.r + self.ident_b.r, writes=bk.r)
        stg = self.rot("KTst")
        S.act(lambda e: e.copy(out=stg.t[:, :, 0:TS], in_=bkb[0:64, :].rearrange("p (a n) -> p a n", n=128)[:, :, 0:TS]),
              reads=bk.r, writes=stg.r)
        dst = ktscr.rearrange("h m d k -> d (h m) k")[:, hm0:hm0 + 8, key0:key0 + TS]
        S.dma("sp", lambda e: e.dma_start(out=dst, in_=stg.t[:, :, 0:TS]), reads=stg.r + [rscr[0]])

    def qkv(self, NT, TS, nsub, k_dram, v_dram, row0, ktscr, vscr, key0, blk0, rscr):
        S = self.S
        pending = []

        def tick(force=False):
            for ent in list(pending):
                ent[0] -= 1
                if ent[0] <= 0 or force:
                    ent[1]()
                    pending.remove(ent)

        for cg in range(6):
            w, wr = self.wnext("in")
            kind = "qkv"[cg // 2]
            half = cg % 2
            for i in range(nsub):
                bk = self.bank()
                for c in range(8):
                    S.pe(lambda e, c=c, bk=bk, w=w, i=i: e.matmul(bk.t[0:TS, :], lhsT=self.hT.t[:, c, i * TS:(i + 1) * TS],
                                                                  rhs=w[:, c, :], start=(c == 0), stop=(c == 7)),
                         reads=wr + [self.hT.r[c]], writes=bk.r)
                tick()
                if kind == "v":
                    vf = self.rot("vf")
                    vb = self.rot("vst")
                    S.act(lambda e, bk=bk, vf=vf: e.copy(out=vf.t[0:TS, :], in_=bk.t[0:TS, :]), reads=bk.r, writes=vf.r)
                    S.dve(lambda e, vf=vf, vb=vb: e.tensor_copy(out=vb.t[0:TS, :], in_=vf.t[0:TS, :]), reads=vf.r, writes=vb.r)
                    self.fin.append(S.dma("sp", lambda e, vf=vf, i=i, half=half: e.dma_start(
                        out=v_dram[row0 + i * TS:row0 + (i + 1) * TS, half * 512:(half + 1) * 512], in_=vf.t[0:TS, :]),
                        reads=vf.r))
                    S.dma("sp", lambda e, vb=vb, i=i, half=half: e.dma_start(
                        out=vscr[0:TS, blk0 + i, half * 512:(half + 1) * 512], in_=vb.t[0:TS, :]), reads=vb.r + [rscr[1]])
                    continue
                sq = self.scr()
                S.act(lambda e, bk=bk, sq=sq: e.activation(out=sq.t[0:TS, :], in_=bk.t[0:TS, :], func=AF.Square),
                      reads=bk.r, writes=sq.r)
                ss = self.st8n()
                S.dve(lambda e, sq=sq, ss=ss: e.tensor_reduce(out=ss.t[0:TS, :], in_=sq.t[0:TS, :].rearrange("p (g d) -> p g d", d=64),
                                                              axis=AX.X, op=ALU.add), reads=sq.r, writes=ss.r)
                S.act(lambda e, ss=ss: e.activation(out=ss.t[0:TS, :], in_=ss.t[0:TS, :], func=AF.Sqrt, bias=self.epsT.t[0:TS, :],
                                                    scale=1.0 / 64), reads=ss.r + self.epsT.r, writes=ss.r)
                S.dve(lambda e, ss=ss: e.reciprocal(out=ss.t[0:TS, :], in_=ss.t[0:TS, :]), reads=ss.r, writes=ss.r)
                t = self.scr()
                S.dve(lambda e, bk=bk, ss=ss, t=t: e.tensor_tensor(
                    out=t.t[0:TS, :].rearrange("p (g d) -> p g d", d=64),
                    in0=bk.t[0:TS, :].rearrange("p (g d) -> p g d", d=64),
                    in1=ss.t[0:TS, :].rearrange("p (g o) -> p g o", o=1).to_broadcast([TS, 8, 64]),
                    op=ALU.mult), reads=bk.r + ss.r, writes=t.r)
                if kind == "k":
                    kf = self.rot("knf")
                    kb = self.rot("knb")
                    S.dve(lambda e, t=t, kf=kf: e.tensor_tensor(
                        out=kf.t[0:TS, :].rearrange("p (g d) -> p g d", d=64),
                        in0=t.t[0:TS, :].rearrange("p (g d) -> p g d", d=64),
                        in1=self.gkb.t[0:TS, :].rearrange("p (o d) -> p o d", o=1).to_broadcast([TS, 8, 64]),
                        op=ALU.mult), reads=t.r + self.gkb.r, writes=kf.r)
                    S.act(lambda e, kf=kf, kb=kb: e.copy(out=kb.t[0:TS, :], in_=kf.t[0:TS, :]), reads=kf.r, writes=kb.r)
                    self.fin.append(S.dma("sp", lambda e, kf=kf, i=i, half=half: e.dma_start(
                        out=k_dram[row0 + i * TS:row0 + (i + 1) * TS, half * 512:(half + 1) * 512], in_=kf.t[0:TS, :]),
                        reads=kf.r))
                    pending.append([3, lambda kb=kb, half=half, i=i: self.kT_store(kb, TS, half * 8, ktscr, key0 + i * TS, rscr)])
                else:
                    qb = self.rot("qnb")
                    S.dve(lambda e, t=t, qb=qb: e.tensor_tensor(
                        out=qb.t[0:TS, :].rearrange("p (g d) -> p g d", d=64),
                        in0=t.t[0:TS, :].rearrange("p (g d) -> p g d", d=64),
                        in1=self.gq8.t[0:TS, :].rearrange("p (o d) -> p o d", o=1).to_broadcast([TS, 8, 64]),
                        op=ALU.mult), reads=t.r + self.gq8.r, writes=qb.r)

                    def q_tr(qb=qb, half=half, i=i):
                        bq = self.bank()
                        bqb = bq.t[:].bitcast(BF16)
                        for a in range(8):
                            S.pe(lambda e, a=a, qb=qb, bqb=bqb: e.transpose(out=bqb[0:64, a * 128:a * 128 + TS],
                                                                            in_=qb.t[0:TS, a * 64:(a + 1) * 64],
                                                                            identity=self.ident_b.t[0:TS, 0:TS]),
                                 reads=qb.r + self.ident_b.r, writes=bq.r)
                        S.act(lambda e, bqb=bqb, i=i, half=half: e.copy(
                            out=self.QT.t[0:64, half * 8:(half + 1) * 8, i * TS:(i + 1) * TS],
                            in_=bqb[0:64, :].rearrange("p (a n) -> p a n", n=128)[:, :, 0:TS]),
                            reads=bq.r, writes=self.QT.r[half * 8:(half + 1) * 8])
                    pending.append([3, q_tr])
        tick(force=True)

    def attention(self, NT, chunks_by_head, ktscr, vscr):
        S = self.S
        accO = [self.pb[0], self.pb[1]]
        accS = [self.pb[2], self.pb[3]]
        valid = []
        loads = []
        for h in range(H):
            for ci, ch in enumerate(chunks_by_head[h]):
                loads.append((h, ci))
                nb = (ch["nk"] + 127) // 128
                for j in range(nb):
                    if ch["diag"] and 128 * j >= NT:
                        continue
                    valid.append((h, ci, j))
        load_idx = {k: n for n, k in enumerate(loads)}
        slot_of = {}
        state = {"issued": 0}

        def issue_load():
            n = state["issued"]
            h, ci = loads[n]
            ch = chunks_by_head[h][ci]
            sl = n % 4
            slot_of[(h, ci)] = sl
            nk = ch["nk"]
            nb = (nk + 127) // 128
            srcK = ktscr[h].rearrange("m d k -> d m k")[:, :, ch["key0"]:ch["key0"] + nk]
            S.dma("sp", lambda e, sl=sl, srcK=srcK, nk=nk: e.dma_start(out=self.slotK[sl].t[0:64, :, 0:nk], in_=srcK),
                  writes=[ch["rscr"][0], self.r_slotK[sl]])
            kl = min(128, nk)
            srcV = vscr[0:kl, ch["blk0"]:ch["blk0"] + nb, h * 128:(h + 1) * 128]
            S.dma("sp", lambda e, sl=sl, srcV=srcV, kl=kl, nb=nb: e.dma_start(out=self.slotV[sl].t[0:kl, 0:nb, :], in_=srcV),
                  writes=[ch["rscr"][1], self.r_slotV[sl]])
            state["issued"] = n + 1

        def emit_S(h, ci, j):
            ch = chunks_by_head[h][ci]
            idx = load_idx[(h, ci)]
            while state["issued"] < min(len(loads), idx + 3):
                issue_load()
            sl = slot_of[(h, ci)]
            nkb = min(128, ch["nk"] - 128 * j)
            qlo = 128 * j if ch["diag"] else 0
            dist = ch["dist0"] - j
            spair = self.SP[self._sb_i % 2]
            self._sb_i += 1
            for m in range(2):
                sbk_t = spair.t[:, m, :]
                sbk_r = [spair.r[m]]
                rq = [self.QT.r[2 * h + m], self.r_qaug, self.r_slotK[sl], self.r_slot_ones[sl]]
                S.pe(lambda e, m=m, sbk_t=sbk_t, sl=sl, nkb=nkb, qlo=qlo, j=j, h=h, diag=ch["diag"]: e.matmul(
                    sbk_t[0:nkb, qlo:NT], lhsT=self.slotK[sl].t[0:66, m, j * 128:j * 128 + nkb],
                    rhs=self.QT.t[0:66, 2 * h + m, qlo:NT], start=True, stop=(not diag)),
                    reads=rq, writes=sbk_r)
                if ch["diag"]:
                    cw = min(128, NT - qlo)
                    S.pe(lambda e, sbk_t=sbk_t, nkb=nkb, qlo=qlo, cw=cw, h=h: e.matmul(
                        sbk_t[0:nkb, qlo:qlo + cw], lhsT=self.ident_b.t[0:nkb, 0:nkb], rhs=self.cmask.t[0:nkb, h, 0:cw],
                        start=False, stop=True), reads=self.ident_b.r + self.cmask.r, writes=sbk_r)
            pt = self.PT[self._pt_i % 2]
            self._pt_i += 1
            col = h * NDIST + dist + 3
            S.act(lambda e, spair=spair, pt=pt, nkb=nkb, qlo=qlo, col=col: e.activation(
                out=pt.t[0:nkb, :, qlo:NT], in_=spair.t[0:nkb, :, qlo:NT], func=AF.Exp, bias=self.biasH.t[0:nkb, col:col + 1],
                scale=1.0), reads=spair.r + self.biasH.r, writes=pt.r)
            pts = pt
            return (h, ci, j, sl, nkb, qlo, pts)

        def emit_AV(item, first, last):
            h, ci, j, sl, nkb, qlo, pt = item
            for m in range(2):
                S.pe(lambda e, m=m, sl=sl, nkb=nkb, qlo=qlo, j=j, pt=pt: e.matmul(
                    accO[m].t[:, qlo:NT], lhsT=self.slotV[sl].t[0:nkb, j, :], rhs=pt.t[0:nkb, m, qlo:NT],
                    start=first, stop=last), reads=[self.r_slotV[sl]] + pt.r, writes=accO[m].r)
                S.pe(lambda e, m=m, nkb=nkb, qlo=qlo, pt=pt: e.matmul(
                    accS[m].t[:, qlo:NT], lhsT=self.ones_b.t[0:nkb, :], rhs=pt.t[0:nkb, m, qlo:NT],
                    start=first, stop=last), reads=self.ones_b.r + pt.r, writes=accS[m].r)

        def epilogue_a(h):
            cO = [self.scr(), self.scr()]
            cS = [self.scr(), self.scr()]
            for m in range(2):
                S.dve(lambda e, m=m: e.tensor_copy(out=cS[m].t[:, 0:NT], in_=accS[m].t[:, 0:NT]), reads=accS[m].r, writes=cS[m].r)
                S.dve(lambda e, m=m: e.tensor_copy(out=cO[m].t[:, 0:NT], in_=accO[m].t[:, 0:NT]), reads=accO[m].r, writes=cO[m].r)
            for m in range(2):
                S.dve(lambda e, m=m: e.reciprocal(out=cS[m].t[:, 0:NT], in_=cS[m].t[:, 0:NT]), reads=cS[m].r, writes=cS[m].r)
                S.dve(lambda e, m=m: e.tensor_tensor(out=cO[m].t[:, 0:NT], in0=cO[m].t[:, 0:NT], in1=cS[m].t[:, 0:NT],
                                                     op=ALU.mult), reads=cO[m].r + cS[m].r, writes=cO[m].r)
            od_t = self.gzf.t[:, (h % 2) * 512:(h % 2) * 512 + 512]
            od_r = [self._r_od[h % 2]]
            osq = self.osq[h % 2]
            S.dve(lambda e: e.scalar_tensor_tensor(out=od_t[:, 0:NT], in0=cO[1].t[:, 0:NT], scalar=self.neg_lam.t[:, 0:1],
                                                   in1=cO[0].t[:, 0:NT], op0=ALU.mult, op1=ALU.add),
                  reads=cO[0].r + cO[1].r + self.neg_lam.r, writes=od_r)
            S.dve(lambda e: e.tensor_tensor(out=osq.t[:, 0:NT], in0=od_t[:, 0:NT], in1=od_t[:, 0:NT], op=ALU.mult),
                  reads=od_r, writes=osq.r)
            return (h, (od_t, od_r, osq))

        def epilogue_b(h, st):
            od_t, od_r, osq = st
            spair = self.SP[self._sb_i % 2]
            self._sb_i += 1
            bn = TB(spair.t[:, 0, :], 0)
            bn.r = [spair.r[0]]
            S.pe(lambda e: e.matmul(bn.t[:, 0:NT], lhsT=self.ones_b.t[:], rhs=osq.t[:, 0:NT], start=True, stop=True),
                 reads=osq.r + self.ones_b.r, writes=bn.r)
            rt = self.scr()
            S.act(lambda e: e.activation(out=rt.t[:, 0:NT], in_=bn.t[:, 0:NT], func=AF.Ln, bias=self.epsT.t[:], scale=1.0 / 128),
                  reads=bn.r + self.epsT.r, writes=rt.r)
            rstd = self.scr()
            S.act(lambda e: e.activation(out=rstd.t[:, 0:NT], in_=rt.t[:, 0:NT], func=AF.Exp, scale=-0.5), reads=rt.r, writes=rstd.r)
            S.dve(lambda e: e.scalar_tensor_tensor(out=self.oT.t[:, h, 0:NT], in0=od_t[:, 0:NT], scalar=self.subgs.t[:, 0:1],
                                                   in1=rstd.t[:, 0:NT], op0=ALU.mult, op1=ALU.mult),
                  reads=od_r + rstd.r + self.subgs.r, writes=[self.oT.r[h]])

        per_head = {}
        for (h, ci, j) in valid:
            per_head.setdefault(h, []).append((ci, j))
        pending_b = []

        def tick_b(force=False):
            for ent in list(pending_b):
                ent[0] -= 1
                if ent[0] <= 0 or force:
                    epilogue_b(ent[1], ent[2])
                    pending_b.remove(ent)

        prev_item = None
        for n, (h, ci, j) in enumerate(valid):
            it = emit_S(h, ci, j)
            if prev_item is not None:
                ph, pci, pj = prev_item[0], prev_item[1], prev_item[2]
                last = (pci, pj) == per_head[ph][-1]
                emit_AV(prev_item, (pci, pj) == per_head[ph][0], last)
                tick_b()
                if last:
                    tick_b(force=True)
                    pending_b.append([10] + list(epilogue_a(ph)))
            prev_item = it
        ph, pci, pj = prev_item[0], prev_item[1], prev_item[2]
        emit_AV(prev_item, (pci, pj) == per_head[ph][0], (pci, pj) == per_head[ph][-1])
        tick_b(force=True)
        epilogue_b(*epilogue_a(ph))

    def mix_out(self, NT, TS, nsub, s, gs_dram):
        S = self.S
        sT = lambda g: (self.big.t[:, g, 0:NT], self.big.r[g])
        mixT = lambda c: (self.big.t[:, 8 + c, 0:NT], self.big.r[8 + c])
        uT = lambda c: (self.big.t[:, 16 + c, 0:NT], self.big.r[16 + c])
        gvn_t = self.big.t[:, 24:32, :].rearrange("p (i a) n -> p i (a n)", a=2)
        gvn_r = lambda i: self.big.r[24 + 2 * i:26 + 2 * i]
        (w0, wr0), (w1, wr1) = self.wgroup(["gv", "gv"])
        for i in range(nsub):
            for cg, (w, wr) in enumerate(((w0, wr0), (w1, wr1))):
                bk = self.bank()
                for kc in range(8):
                    S.pe(lambda e, kc=kc, bk=bk, w=w, i=i: e.matmul(bk.t[0:TS, :], lhsT=self.hT.t[:, kc, i * TS:(i + 1) * TS],
                                                                    rhs=w[:, kc, :], start=(kc == 0), stop=(kc == 7)),
                         reads=wr + [self.hT.r[kc]], writes=bk.r)
                S.act(lambda e, bk=bk, cg=cg: e.activation(out=self.gzf.t[0:TS, cg * 512:(cg + 1) * 512], in_=bk.t[0:TS, :],
                                                           func=AF.Gelu_apprx_tanh), reads=bk.r, writes=self.gzf.r)
            ss = self.st8n()
            junk = self.scr()
            for cg in range(2):
                S.act(lambda e, cg=cg, ss=ss, junk=junk: e.activation(out=junk.t[0:TS, :], in_=self.gzf.t[0:TS, cg * 512:(cg + 1) * 512],
                                                                      func=AF.Square, accum_out=ss.t[0:TS, cg:cg + 1]),
                      reads=self.gzf.r, writes=junk.r + ss.r)
            S.dve(lambda e, ss=ss: e.tensor_tensor(out=ss.t[0:TS, 2:3], in0=ss.t[0:TS, 0:1], in1=ss.t[0:TS, 1:2], op=ALU.add),
                  reads=ss.r, writes=ss.r)
            S.act(lambda e, ss=ss: e.activation(out=ss.t[0:TS, 3:4], in_=ss.t[0:TS, 2:3], func=AF.Sqrt, bias=self.epsT.t[0:TS, :],
                                                scale=1.0 / D), reads=ss.r + self.epsT.r, writes=ss.r)
            S.dve(lambda e, ss=ss: e.reciprocal(out=ss.t[0:TS, 4:5], in_=ss.t[0:TS, 3:4]), reads=ss.r, writes=ss.r)
            S.dve(lambda e, ss=ss, i=i: e.scalar_tensor_tensor(out=gvn_t[0:TS, i, :], in0=self.gzf.t[0:TS, :], scalar=ss.t[0:TS, 4:5],
                                                               in1=self.vngb.t[0:TS, :], op0=ALU.mult, op1=ALU.mult),
                  reads=self.gzf.r + ss.r + self.vngb.r, writes=gvn_r(i))
            if gs_dram is not None:
                S.dve(lambda e, ss=ss: e.scalar_tensor_tensor(out=self.gzf.t[0:TS, :], in0=self.gzf.t[0:TS, :], scalar=ss.t[0:TS, 4:5],
                                                              in1=self.vngb.t[0:TS, :], op0=ALU.mult, op1=ALU.mult),
                      reads=self.gzf.r + ss.r + self.vngb.r, writes=self.gzf.r)
                self.fin.append(S.dma("sp", lambda e: e.dma_start(out=gs_dram[0:TS, :], in_=self.gzf.t[0:TS, :]), reads=self.gzf.r))
        for cg in range(2):
            w, wr = self.wnext("u")
            for k in range(4):
                c = cg * 4 + k
                bk = self.bank()
                for kc in range(8):
                    S.pe(lambda e, kc=kc, bk=bk, w=w, k=k: e.matmul(bk.t[:, 0:NT], lhsT=w[:, kc, k * 128:(k + 1) * 128],
                                                                    rhs=self.hT.t[:, kc, 0:NT], start=(kc == 0), stop=(kc == 7)),
                         reads=wr + [self.hT.r[kc]], writes=bk.r)
                S.act(lambda e, bk=bk, c=c: e.activation(out=uT(c)[0], in_=bk.t[:, 0:NT], func=AF.Gelu_apprx_tanh),
                      reads=bk.r, writes=[uT(c)[1]])
        for g in range(8):
            bk = self.bank()
            for i in range(nsub):
                S.pe(lambda e, g=g, i=i, bk=bk: e.matmul(bk.t[:, i * TS:(i + 1) * TS], lhsT=gvn_t[0:TS, i, g * 128:(g + 1) * 128],
                                                         rhs=self.wsT.t[0:TS, g, 0:TS], start=True, stop=True),
                     reads=gvn_r(i) + self.wsT.r, writes=bk.r)
            t = self.scr()
            S.dve(lambda e, g=g, bk=bk, t=t: e.tensor_tensor(
                out=t.t[:, 0:NT].rearrange("p (i n) -> p i n", n=TS),
                in0=bk.t[:, 0:NT].rearrange("p (i n) -> p i n", n=TS),
                in1=self.bsb.t[:, g, 0:TS].rearrange("p (o n) -> p o n", o=1).to_broadcast([128, nsub, TS]),
                op=ALU.add), reads=bk.r + self.bsb.r, writes=t.r)
            S.dve(lambda e, g=g, t=t: e.tensor_tensor(out=sT(g)[0], in0=t.t[:, 0:NT], in1=uT(g)[0], op=ALU.mult),
                  reads=t.r + [uT(g)[1]], writes=[sT(g)[1]])
        for cb in range(2):
            (wA, rA), (wB, rB), (wgA, rgA), (wgB, rgB) = self.wgroup(["brA", "brB", "gA", "gB"])
            for k in range(4):
                mc = cb * 4 + k
                bA = self.bank()
                for kc in range(8):
                    S.pe(lambda e, kc=kc, bA=bA, k=k, wA=wA: e.matmul(bA.t[:, 0:NT], lhsT=wA[:, kc, k * 128:(k + 1) * 128],
                                                                      rhs=self.oT.t[:, kc, 0:NT], start=(kc == 0), stop=(kc == 7)),
                         reads=rA + [self.oT.r[kc]], writes=bA.r)
                bB = self.bank()
                for kc in range(8):
                    S.pe(lambda e, kc=kc, bB=bB, k=k, wB=wB: e.matmul(bB.t[:, 0:NT], lhsT=wB[:, kc, k * 128:(k + 1) * 128],
                                                                      rhs=sT(kc)[0], start=(kc == 0), stop=(kc == 7)),
                         reads=rB + [sT(kc)[1]], writes=bB.r)
                bGA = self.bank()
                for kc in range(8):
                    S.pe(lambda e, kc=kc, bGA=bGA, k=k, wgA=wgA: e.matmul(bGA.t[:, 0:NT], lhsT=wgA[:, kc, k * 128:(k + 1) * 128],
                                                                          rhs=self.hT.t[:, kc, 0:NT], start=(kc == 0), stop=(kc == 7)),
                         reads=rgA + [self.hT.r[kc]], writes=bGA.r)
                bGB = self.bank()
                for kc in range(8):
                    S.pe(lambda e, kc=kc, bGB=bGB, k=k, wgB=wgB: e.matmul(bGB.t[:, 0:NT], lhsT=wgB[:, kc, k * 128:(k + 1) * 128],
                                                                          rhs=self.hT.t[:, kc, 0:NT], start=(kc == 0), stop=(kc == 7)),
                         reads=rgB + [self.hT.r[kc]], writes=bGB.r)
                gA = self.scr()
                S.act(lambda e, bGA=bGA, gA=gA, mc=mc: e.activation(out=gA.t[:, 0:NT], in_=bGA.t[:, 0:NT], func=AF.Sigmoid,
                                                                    bias=self.bgate.t[:, mc:mc + 1], scale=1.0),
                      reads=bGA.r + self.bgate.r, writes=gA.r)
                gB = self.scr()
                S.act(lambda e, bGB=bGB, gB=gB, mc=mc: e.activation(out=gB.t[:, 0:NT], in_=bGB.t[:, 0:NT], func=AF.Sigmoid,
                                                                    bias=self.bgate.t[:, 8 + mc:9 + mc], scale=1.0),
                      reads=bGB.r + self.bgate.r, writes=gB.r)
                S.dve(lambda e, bA=bA, gA=gA: e.tensor_tensor(out=gA.t[:, 0:NT], in0=bA.t[:, 0:NT], in1=gA.t[:, 0:NT], op=ALU.mult),
                      reads=bA.r + gA.r, writes=gA.r)
                S.dve(lambda e, bB=bB, gB=gB: e.tensor_tensor(out=gB.t[:, 0:NT], in0=bB.t[:, 0:NT], in1=gB.t[:, 0:NT], op=ALU.mult),
                      reads=bB.r + gB.r, writes=gB.r)
                S.dve(lambda e, gA=gA, gB=gB, mc=mc: e.tensor_tensor(out=mixT(mc)[0], in0=gA.t[:, 0:NT], in1=gB.t[:, 0:NT], op=ALU.add),
                      reads=gA.r + gB.r, writes=[mixT(mc)[1]])
        for cb in range(2):
            w, wr = self.wnext("out")
            for k in range(4):
                mc = cb * 4 + k
                bk = self.bank()
                for kc in range(8):
                    S.pe(lambda e, kc=kc, bk=bk, w=w, k=k: e.matmul(bk.t[:, 0:NT], lhsT=w[:, kc, k * 128:(k + 1) * 128],
                                                                    rhs=mixT(kc)[0], start=(kc == 0), stop=(kc == 7)),
                         reads=wr + [mixT(kc)[1]], writes=bk.r)
                S.dve(lambda e, bk=bk, mc=mc: e.scalar_tensor_tensor(out=self.xT.t[:, mc, 0:NT], in0=bk.t[:, 0:NT],
                                                                     scalar=self.dvs(s, 7, mc), in1=self.xT.t[:, mc, 0:NT],
                                                                     op0=ALU.mult, op1=ALU.add),
                      reads=bk.r + self.dv.r + [self.xT.r[mc]], writes=[self.xT.r[mc]])

    def tile(self, kind, t=0):
        stage = getattr(self, "stage", 99)
        if kind == "prompt":
            NT, TS, nsub, s = 512, 128, 4, 0
            x_dram, y_dram, k_dram, v_dram, gs_dram = self.xp, self.yp, self.kp, self.vp, None
            row0 = 512 * t
            ktscr, vscr = self.ktscr_p, self.vscr_p
            key0, blk0, rscr = 512 * t, 4 * t, self.r_scr_p[t]
            q0 = 512 * t
            chunks = [dict(key0=512 * c, nk=512, blk0=4 * c, rscr=self.r_scr_p[c], diag=(c == t), dist0=4 * (t - c))
                      for c in range(t + 1)]
        else:
            NT, TS, nsub, s = 16, 16, 1, 1
            x_dram, y_dram, k_dram, v_dram, gs_dram = self.xs, self.ys, self.ks, self.vs, self.gs
            row0 = 0
            ktscr, vscr = self.ktscr_s, self.vscr_s
            key0, blk0, rscr = 1024, 8, self.r_scr_s[2]
            q0 = 1024
            chunks = [dict(key0=0, nk=512, blk0=0, rscr=self.r_scr_s[0], diag=False, dist0=8),
                      dict(key0=512, nk=512, blk0=4, rscr=self.r_scr_s[1], diag=False, dist0=4),
                      dict(key0=1024, nk=16, blk0=8, rscr=self.r_scr_s[2], diag=True, dist0=0)]
        chunks_by_head = []
        for h in range(H):
            slope = 2.0 ** (-(h + 1))
            lst = [ch for ch in chunks
                   if ch["diag"] or slope * (q0 - (ch["key0"] + ch["nk"] - 1)) < ALIBI_SKIP]
            chunks_by_head.append(lst)
        dbg = getattr(self, "debug", False) and kind == getattr(self, "debug_kind", "sample") and t == 0
        self.load_x(x_dram, row0, NT, TS, nsub)
        if dbg:
            self.dump("modT", self.modT.t[:], [128, 72, 2], self.modT.r)
            self.dump("dv", self.dv.t[:], [128, 2, 10, 8], self.dv.r)
            self.dump("xT0", self.xT.t[:, :, 0:NT], [128, 8, NT], self.xT.r)
        if stage < 3:
            return
        self.rmsnorm_mod(NT, s, 0)
        if dbg:
            self.dump("h1", self.hT.t[:, :, 0:NT], [128, 8, NT], self.hT.r, BF16)
        if stage < 4:
            return
        self.ffn(NT, s, 6)
        if dbg:
            self.dump("hid", self.big.t[:, 0:22, 0:NT], [128, 22, NT], self.big.r[0:22], BF16)
            self.dump("xT1", self.xT.t[:, :, 0:NT], [128, 8, NT], self.xT.r)
        self.rmsnorm_mod(NT, s, 1)
        if dbg:
            self.dump("h2", self.hT.t[:, :, 0:NT], [128, 8, NT], self.hT.r, BF16)
        if stage < 5:
            return
        self.qkv(NT, TS, nsub, k_dram, v_dram, row0, ktscr, vscr, key0, blk0, rscr)
        if stage < 6:
            return
        if dbg:
            self.dump("QT", self.QT.t[:, :, 0:NT], [66, 16, NT], self.QT.r + [self.r_qaug], BF16)
        self.attention(NT, chunks_by_head, ktscr, vscr)
        nxt = getattr(self, "_next_tile", None)
        if nxt is not None:
            self.prefetch_x(self.xp, 512 * nxt, 128, 4)
        if dbg:
            self.dump("oT", self.oT.t[:, :, 0:NT], [128, 8, NT], self.oT.r, BF16)
        if stage < 7:
            return
        self.mix_out(NT, TS, nsub, s, gs_dram)
        if dbg:
            self.dump("sT", self.big.t[:, 0:8, 0:NT], [128, 8, NT], self.big.r[0:8], BF16)
            self.dump("mixT", self.big.t[:, 8:16, 0:NT], [128, 8, NT], self.big.r[8:16], BF16)
            self.dump("xT2", self.xT.t[:, :, 0:NT], [128, 8, NT], self.xT.r)
        self.rmsnorm_mod(NT, s, 2)
        self.ffn(NT, s, 8)
        self.store_y(y_dram, row0, NT, TS, nsub, s)

    def sample_cache_prep(self):
        S = self.S
        for half in range(2):
            S.dma("pool", lambda e, half=half: e.dma_start(
                out=self.vscr_s[:, 4 * half:4 * half + 4, :],
                in_=self.cv[512 * half:512 * half + 512, :].rearrange("(blk kl) n -> kl blk n", kl=128)),
                reads=[self.r_scr_s[half][1]])
        for blk in range(8):
            for half in range(2):
                kf = self.rot("knf")
                kb = self.rot("knb")
                S.dma("sp", lambda e, kf=kf, blk=blk, half=half: e.dma_start(
                    out=kf.t[:, :], in_=self.ck[blk * 128:(blk + 1) * 128, half * 512:(half + 1) * 512]), writes=kf.r)
                S.dve(lambda e, kf=kf, kb=kb: e.tensor_copy(out=kb.t[:, :], in_=kf.t[:, :]), reads=kf.r, writes=kb.r)
                self.kT_store(kb, 128, half * 8, self.ktscr_s, blk * 128, self.r_scr_s[blk // 4])

    def build(self):
        self.declare()
        self._sb_i = 0
        ntiles_total = self.n_prompt_tiles + (1 if self.do_sample else 0)
        self.plan_weights(ntiles_total)
        stage = getattr(self, "stage", 99)
        self.prologue()
        if self.do_sample and stage >= 1:
            self.sample_cache_prep()
            if stage >= 2:
                self._next_tile = 0 if self.n_prompt_tiles > 0 else None
                self.tile("sample")
        for t in range(self.n_prompt_tiles):
            self._next_tile = t + 1 if t + 1 < self.n_prompt_tiles else None
            self.tile("prompt", t)
        self.S.emit(final_wait_ops=self.fin)
        self.st.close()
        return self.nc


def _consts():
    slopes = 2.0 ** (-8.0 * np.arange(1, H + 1) / H)
    kl = np.arange(128)[:, None, None]
    di = np.arange(NDIST)[None, None, :]
    biasH = slopes[None, :, None] * (kl - 128.0 * (di - 3))
    biasH = biasH.reshape(128, H * NDIST).astype(np.float32)
    klm = np.arange(128)[:, None]
    qq = np.arange(128)[None, :]
    base = np.where(qq >= klm, 0.0, -2.0 * (klm - qq))
    same = (klm // 64) == (qq // 64)
    cm = np.zeros((128, H, 128), np.float64)
    for h in range(H):
        c = slopes[h] * base
        c = np.where((qq < klm) & (~same), -30000.0, c)
        cm[:, h, :] = c
    cmask = cm.reshape(128, H * 128).astype(np.float32)
    qp = np.arange(512)
    qaug = np.zeros((2, 16, 512), np.float64)
    for h in range(H):
        for m in range(2):
            qaug[0, 2 * h + m] = -slopes[h] * 64.0 * (qp // 64)
            qaug[1, 2 * h + m] = -slopes[h] * (qp % 64)
    qaug = qaug.reshape(2, 16 * 512).astype(np.float32)
    s_ = np.arange(128)[:, None]
    t_ = np.arange(128)[None, :]
    tril = (s_ <= t_).astype(np.float32)
    return dict(c_ident=np.eye(128, dtype=np.float32), c_biasH=biasH, c_cmask=cmask, c_qaug=qaug, c_tril=tril)


def _fm(v, n):
    return np.ascontiguousarray(np.asarray(v, np.float32).reshape(n, 128).T)


_NC_CACHE = {}


def kernel(x_prompt, x_sample, cache_k, cache_v, c_prompt, c_sample, ada_w, ada_b, norm_g,
           ffn1_wgu, ffn1_wd, w_in, q_norm_g, k_norm_g, lambda_qk, attn_subln_g, gmlp_vnorm_g,
           gmlp_ws, gmlp_bs, w_gate, b_gate, w_branch, w_out, ffn2_wgu, ffn2_wd,
           _n_prompt_tiles=N_PROMPT_TILES, _do_sample=True, _debug=False, _debug_kind="sample", _stage=99, _ncores=8):
    f = lambda a: np.ascontiguousarray(np.asarray(a, dtype=np.float32))
    x_prompt, x_sample, cache_k, cache_v = f(x_prompt), f(x_sample), f(cache_k), f(cache_v)
    key = (_n_prompt_tiles, _do_sample, _debug, _debug_kind, _stage)
    if key not in _NC_CACHE:
        _NC_CACHE[key] = Builder(_n_prompt_tiles, _do_sample, _debug, _debug_kind, _stage).build()
    nc = _NC_CACHE[key]
    consts = _consts()
    shared = dict(
        ada_w=f(ada_w[0]), ada_bT=_fm(ada_b[0], 72),
        ngT=np.ascontiguousarray(np.asarray(norm_g[0], np.float32).reshape(4, 8, 128).transpose(2, 0, 1).reshape(128, 32)),
        w1gu=f(ffn1_wgu[0]), w1d=f(ffn1_wd[0]), w_in=f(w_in[0]),
        gq_b=np.ascontiguousarray(np.broadcast_to(f(q_norm_g[0])[None, :], (128, 64))),
        gk_b=np.ascontiguousarray(np.broadcast_to(f(k_norm_g[0])[None, :], (128, 64))),
        lq_b=np.ascontiguousarray(np.broadcast_to(f(lambda_qk[0]).reshape(1, 256), (128, 256))),
        sublnT=f(attn_subln_g[0]).reshape(128, 1),
        vng_b=np.ascontiguousarray(np.broadcast_to(f(gmlp_vnorm_g[0])[None, :], (128, D))),
        ws=f(gmlp_ws[0]),
        bs_b=np.ascontiguousarray(np.broadcast_to(f(gmlp_bs[0]).reshape(1, D), (128, D))),
        w_gate=f(w_gate[0]), bgT=_fm(b_gate[0], 16), w_br=f(w_branch[0]), w_out=f(w_out[0]),
        w2gu=f(ffn2_wgu[0]), w2d=f(ffn2_wd[0]),
    )
    shared.update(consts)
    in_maps = []
    for b in range(8):
        m = dict(shared)
        m["xp"] = x_prompt[b]
        m["xs"] = x_sample[b]
        m["ck"] = cache_k[0, b].reshape(PAST, D)
        m["cv"] = cache_v[0, b].reshape(PAST, D)
        cv2 = np.stack([f(c_prompt[b]), f(c_sample[b])], axis=-1)
        m["cvecT"] = np.ascontiguousarray(cv2.reshape(8, 128, 2).transpose(1, 0, 2).reshape(128, 16))
        in_maps.append(m)
    if _ncores != 8:
        in_maps = in_maps[:_ncores]
    res = run_bass_kernel_spmd(nc, in_maps, core_ids=list(range(_ncores)))
    R = list(res.results) + [res.results[0]] * (8 - _ncores)
    if _debug:
        global _DEBUG_RES
        _DEBUG_RES = R
    y_prompt = np.stack([R[b]["yp"] for b in range(8)]).astype(np.float32)
    y_sample = np.stack([R[b]["ys"] for b in range(8)]).astype(np.float32)
    k_prompt = np.stack([R[b]["kp"] for b in range(8)]).reshape(1, 8, SEQ, H, 2, 64).astype(np.float32)
    v_prompt = np.stack([R[b]["vp"] for b in range(8)]).reshape(1, 8, SEQ, H, 128).astype(np.float32)
    k_sample = np.stack([R[b]["ks"] for b in range(8)]).reshape(1, 8, DS, H, 2, 64).astype(np.float32)
    v_sample = np.stack([R[b]["vs"] for b in range(8)]).reshape(1, 8, DS, H, 128).astype(np.float32)
    g_sample = np.stack([R[b]["gs"] for b in range(8)]).reshape(1, 8, DS, D).astype(np.float32)
    return (y_prompt, y_sample, k_prompt, v_prompt, k_sample, v_sample, g_sample)
```

```python
import contextlib
import numpy as np
import concourse.bass as bass
import concourse.mybir as mybir
from concourse.bass_utils import run_bass_kernel_spmd

F32 = mybir.dt.float32
BF16 = mybir.dt.bfloat16
AF = mybir.ActivationFunctionType
ALU = mybir.AluOpType
AX = mybir.AxisListType

D = 1024
SEQ = 8192
DS = 16
PAST = 1024
H = 8
DFF = 2816
NJ = 22
LAM0 = 0.2
EPS = 1e-6
NDIST = 68
ALIBI_SKIP = 168.0
N_PROMPT_TILES = SEQ // 512

ENGS = ("pe", "act", "dve", "pool", "sp")


class Res:
    __slots__ = ("w", "r")

    def __init__(self):
        self.w = None
        self.r = []


class Op:
    __slots__ = ("eng", "fn", "deps", "sig", "dma", "sem", "val")

    def __init__(self, eng, fn, dma):
        self.eng = eng
        self.fn = fn
        self.dma = dma
        self.deps = []
        self.sig = False
        self.sem = None
        self.val = 0


class Sched:
    def __init__(self, nc, n_dma_sems=12, same_engine_waits=True):
        self.nc = nc
        self.ops = {e: [] for e in ENGS}
        self.n_dma_sems = n_dma_sems
        self.same_engine_waits = same_engine_waits

    def add(self, eng, fn, reads=(), writes=(), dma=False):
        op = Op(eng, fn, dma)
        deps = []
        for r in reads:
            if r.w is not None:
                deps.append(r.w)
        for w in writes:
            if w.w is not None:
                deps.append(w.w)
            deps.extend(w.r)
        seen = set()
        for d in deps:
            if id(d) not in seen:
                seen.add(id(d))
                op.deps.append(d)
        for r in reads:
            if not dma:
                r.r = [x for x in r.r if x.dma or x.eng != eng]
            r.r.append(op)
        for w in writes:
            w.w = op
            w.r = []
        self.ops[eng].append(op)
        return op

    def pe(self, fn, reads=(), writes=()):
        return self.add("pe", fn, reads, writes)

    def act(self, fn, reads=(), writes=()):
        return self.add("act", fn, reads, writes)

    def dve(self, fn, reads=(), writes=()):
        return self.add("dve", fn, reads, writes)

    def dma(self, q, fn, reads=(), writes=()):
        return self.add(q, fn, reads, writes, dma=True)

    def emit(self, final_wait_ops=()):
        nc = self.nc
        sew = self.same_engine_waits
        for e in ENGS:
            for op in self.ops[e]:
                for d in op.deps:
                    if d.dma or op.dma or d.eng != op.eng or (sew and op.eng != "pe"):
                        d.sig = True
        for op in final_wait_ops:
            op.sig = True
        with contextlib.ExitStack() as st:
            esem = {e: st.enter_context(nc.semaphore("s_" + e)) for e in ENGS}
            dsem = {}
            for q in ("sp", "pool"):
                dsem[q] = [st.enter_context(nc.semaphore("d_%s%d" % (q, i)))
                           for i in range(self.n_dma_sems)]
            for e in ENGS:
                c = 0
                di = 0
                dcnt = [0] * self.n_dma_sems
                prev = [None] * self.n_dma_sems
                for op in self.ops[e]:
                    if op.dma:
                        s = di % self.n_dma_sems
                        di += 1
                        dcnt[s] += 16
                        op.sem = dsem[e][s]
                        op.val = dcnt[s]
                        if prev[s] is not None:
                            op.deps.append(prev[s])
                        prev[s] = op
                        op.sig = True
                    elif op.sig:
                        c += 1
                        op.sem = esem[e]
                        op.val = c
            block = st.enter_context(nc.Block())

            def run_engine(e, eng):
                seen = {}
                for op in self.ops[e]:
                    for d in op.deps:
                        if d.sem is None:
                            continue
                        if (not d.dma) and (not op.dma) and d.eng == e:
                            if e == "pe" or not sew:
                                continue
                        key = id(d.sem)
                        if seen.get(key, 0) >= d.val:
                            continue
                        seen[key] = d.val
                        eng.wait_ge(d.sem, d.val)
                    ins = op.fn(eng)
                    if op.sig:
                        ins.then_inc(op.sem, 16 if op.dma else 1)
                if e == "sp":
                    for op in final_wait_ops:
                        eng.wait_ge(op.sem, op.val)

            @block.tensor
            def _(eng):
                run_engine("pe", eng)

            @block.scalar
            def _(eng):
                run_engine("act", eng)

            @block.vector
            def _(eng):
                run_engine("dve", eng)

            @block.gpsimd
            def _(eng):
                run_engine("pool", eng)

            @block.sync
            def _(eng):
                run_engine("sp", eng)


class TB:
    def __init__(self, t, n=1):
        self.t = t
        self.r = [Res() for _ in range(n)]


class Builder:
    def __init__(self, n_prompt_tiles=N_PROMPT_TILES, do_sample=True, debug=False, debug_kind="sample", stage=99):
        self.stage = int(stage)
        self.substage = int(round((stage - int(stage)) * 10)) if stage != int(stage) else 99
        self.debug = debug
        self.debug_kind = debug_kind
        self.n_prompt_tiles = n_prompt_tiles
        self.do_sample = do_sample
        self.nc = bass.Bass("TRN2", target_bir_lowering=False)
        self.st = contextlib.ExitStack()
        self.S = Sched(self.nc)
        self.fin = []
        self._scr_i = 0
        self._pb_i = 0
        self._reserved = set()
        self._wblocks = []
        self._w_issued = 0
        self._w_cursor = 0
        self._rr = 0

    def din(self, name, shape, dt=F32):
        return self.nc.dram_tensor(name, list(shape), dt, kind="ExternalInput").ap()

    def dout(self, name, shape, dt=F32):
        return self.nc.dram_tensor(name, list(shape), dt, kind="ExternalOutput").ap()

    def dscr(self, name, shape, dt):
        return self.nc.dram_tensor(name, list(shape), dt).ap()

    def sb(self, name, shape, dt, n=1):
        return TB(self.st.enter_context(self.nc.sbuf_tensor(name, list(shape), dt)), n)

    def ps(self, name, shape, dt):
        return TB(self.st.enter_context(self.nc.psum_tensor(name, list(shape), dt)), 1)

    def scr(self):
        b = self.scrs[self._scr_i % len(self.scrs)]
        self._scr_i += 1
        return b

    def bank(self):
        while (self._pb_i % 8) in self._reserved:
            self._pb_i += 1
        b = self.pb[self._pb_i % 8]
        self._pb_i += 1
        return b

    def alt(self):
        self._rr += 1
        return self._rr % 2

    def declare(self):
        nc = self.nc
        self.xp = self.din("xp", [SEQ, D])
        self.xs = self.din("xs", [DS, D])
        self.ck = self.din("ck", [PAST, D])
        self.cv = self.din("cv", [PAST, D])
        self.cvecT = self.din("cvecT", [128, 16])
        self.ada_w = self.din("ada_w", [D, 9 * D])
        self.ada_bT = self.din("ada_bT", [128, 72])
        self.ngT = self.din("ngT", [128, 32])
        self.w1gu = self.din("w1gu", [D, 2 * DFF])
        self.w1d = self.din("w1d", [DFF, D])
        self.w_in = self.din("w_in", [D, 5 * D])
        self.gq_b = self.din("gq_b", [128, 64])
        self.gk_b = self.din("gk_b", [128, 64])
        self.lq_b = self.din("lq_b", [128, 256])
        self.sublnT = self.din("sublnT", [128, 1])
        self.vng_b = self.din("vng_b", [128, D])
        self.ws = self.din("ws", [8, 128, 128])
        self.bs_b = self.din("bs_b", [128, D])
        self.w_gate = self.din("w_gate", [D, 2 * D])
        self.bgT = self.din("bgT", [128, 16])
        self.w_br = self.din("w_br", [2 * D, D])
        self.w_out = self.din("w_out", [D, D])
        self.w2gu = self.din("w2gu", [D, 2 * DFF])
        self.w2d = self.din("w2d", [DFF, D])
        self.c_ident = self.din("c_ident", [128, 128])
        self.c_biasH = self.din("c_biasH", [128, H * NDIST])
        self.c_cmask = self.din("c_cmask", [128, H * 128])
        self.c_qaug = self.din("c_qaug", [2, 16 * 512])
        self.c_tril = self.din("c_tril", [128, 128])

        self.yp = self.dout("yp", [SEQ, D])
        self.ys = self.dout("ys", [DS, D])
        self.kp = self.dout("kp", [SEQ, D])
        self.vp = self.dout("vp", [SEQ, D])
        self.ks = self.dout("ks", [DS, D])
        self.vs = self.dout("vs", [DS, D])
        self.gs = self.dout("gs", [DS, D])

        self.ktscr_p = self.dscr("ktscr_p", [H, 2, 64, SEQ], BF16)
        self.vscr_p = self.dscr("vscr_p", [128, SEQ // 128, D], BF16)
        self.ktscr_s = self.dscr("ktscr_s", [H, 2, 64, 1536], BF16)
        self.vscr_s = self.dscr("vscr_s", [128, 12, D], BF16)
        self.r_scr_p = [(Res(), Res()) for _ in range(SEQ // 512)]
        self.r_scr_s = [(Res(), Res()) for _ in range(3)]

        sb = self.sb
        self.ident_f = sb("ident_f", [128, 128], F32)
        self.ident_b = sb("ident_b", [128, 128], BF16)
        self.ones_b = sb("ones_b", [128, 128], BF16)
        self.epsT = sb("epsT", [128, 1], F32)
        self.biasH = sb("biasH", [128, H * NDIST], F32)
        self.cmask = sb("cmask", [128, H, 128], BF16)
        self.trilT = sb("trilT", [128, 128], F32)
        self.wsf = None
        self.wsT = sb("wsT", [128, 8, 128], BF16)
        self.bsb = sb("bsb", [128, 8, 128], F32)
        self.vngb = sb("vngb", [128, D], F32)
        self.gq8 = sb("gq8", [128, 64], F32)
        self.gkb = sb("gkb", [128, 64], F32)
        self.lqb = sb("lqb", [128, 256], F32)
        self.lam_t = sb("lam_t", [128, 8], F32)
        self.neg_lam = sb("neg_lam", [128, 1], F32)
        self.ng = sb("ng", [128, 4, 8], F32)
        self.bgate = sb("bgate", [128, 16], F32)
        self.subgs = sb("subgs", [128, 1], F32)
        self.adab = sb("adab", [128, 72], F32)
        self.cvt = sb("cvt", [128, 16], F32)
        self.cs = sb("cs", [128, 16], F32)
        self.modT = sb("modT", [128, 72, 2], F32)
        self.dv = sb("dv", [128, 2, 10, 8], F32)
        self.st8 = [sb("st8_%d" % i, [128, 8], F32) for i in range(4)]
        self._st8_i = 0

        self.xin = [sb("xin%d" % i, [128, D], F32) for i in range(2)]
        self.yout = [sb("yout%d" % i, [128, D], F32) for i in range(2)]
        self.xT = sb("xT", [128, 8, 512], F32, n=8)
        self.sqc = [sb("sqc%d" % i, [128, 512], BF16) for i in range(2)]
        self.hT = sb("hT", [128, 8, 512], BF16, n=8)
        self.big = sb("big", [128, 32, 512], BF16, n=32)
        self.scrs = [sb("scr%d" % i, [128, 512], F32) for i in range(6)]
        self.rstdb = [sb("rstdb%d" % i, [128, 512], F32) for i in range(2)]
        self.knf = [sb("knf%d" % i, [128, 512], F32) for i in range(2)]
        self.knb = [sb("knb%d" % i, [128, 512], BF16) for i in range(3)]
        self.qnb = [sb("qnb%d" % i, [128, 512], BF16) for i in range(3)]
        self.vf = [sb("vf%d" % i, [128, 512], F32) for i in range(2)]
        self.vst = [sb("vst%d" % i, [128, 512], BF16) for i in range(2)]
        self.gzf = sb("gzf", [128, D], F32)
        self.QT = sb("QT", [66, 16, 512], BF16, n=16)
        self.r_qaug = Res()
        self._r_od = self.gzf.r * 2
        self.KTst = [sb("KTst%d" % i, [64, 8, 128], BF16) for i in range(2)]
        self.slotK = [sb("slotK%d" % i, [66, 2, 512], BF16) for i in range(4)]
        self.slotV = [sb("slotV%d" % i, [128, 4, 128], BF16) for i in range(4)]
        self.r_slotK = [Res() for _ in range(4)]
        self.r_slotV = [Res() for _ in range(4)]
        self.r_slot_ones = [Res() for _ in range(4)]
        self.PT = [sb("PT%d" % i, [128, 2, 512], BF16) for i in range(2)]
        self._pt_i = 0
        self.osq = [sb("osq%d" % i, [128, 512], BF16) for i in range(2)]
        self.oT = sb("oT", [128, 8, 512], BF16, n=8)
        self.wsl = [sb("wsl%d" % i, [128, 4096], BF16) for i in range(4)]

        self.pb = [self.ps("pb%d" % i, [128, 512], F32) for i in range(4)]
        self.SP = [self.ps("sp%d" % i, [128, 2, 512], F32) for i in range(2)]
        for i in range(2):
            for m in range(2):
                hb = TB(self.SP[i].t[:, m, :], 1)
                self.pb.append(hb)
            self.SP[i].r = [self.pb[4 + 2 * i].r[0], self.pb[5 + 2 * i].r[0]]
        self._cnt = {k: 0 for k in ("knf", "knb", "qnb", "vf", "vst", "KTst", "xin", "yout", "sqc", "rstdb")}
        print("SBUF bytes remaining per partition:", self.nc.sbuf_bytes_remaining)

    def dump(self, name, ap, shape, reads, dt=F32):
        if not getattr(self, "debug", False):
            return
        d = self.nc.dram_tensor("dbg_" + name, list(shape), dt, kind="ExternalOutput").ap()
        self.fin.append(self.S.dma("sp", lambda e: e.dma_start(out=d, in_=ap), reads=reads))

    def rot(self, name):
        lst = getattr(self, name)
        i = self._cnt[name]
        self._cnt[name] = i + 1
        return lst[i % len(lst)]

    def st8n(self):
        b = self.st8[self._st8_i % 4]
        self._st8_i += 1
        return b

    def plan_weights(self, ntiles_total):
        def blk_cols(w, kc0, nkc, col0, ncols):
            return w[kc0 * 128:(kc0 + nkc) * 128, col0:col0 + ncols].rearrange("(kc p) n -> p kc n", p=128)

        def ffn_blocks(wgu, wd):
            out = []
            for p in range(NJ // 2):
                out.append(("gu", [(0, 256, blk_cols(wgu, 0, 8, 256 * p, 256)),
                                   (256, 256, blk_cols(wgu, 0, 8, DFF + 256 * p, 256))], 8, 512))
            for cp in range(4):
                for half in range(2):
                    out.append(("d", [(0, 256, blk_cols(wd, half * 11, 11, cp * 256, 256))], 11, 256))
            return out

        tmpl = ffn_blocks(self.w1gu, self.w1d)
        for cg in range(6):
            tmpl.append(("in", [(0, 512, blk_cols(self.w_in, 0, 8, cg * 512, 512))], 8, 512))
        for cg in range(2):
            tmpl.append(("gv", [(0, 512, blk_cols(self.w_in, 0, 8, 4096 + cg * 512, 512))], 8, 512))
        for cg in range(2):
            tmpl.append(("u", [(0, 512, blk_cols(self.w_in, 0, 8, 3072 + cg * 512, 512))], 8, 512))
        for cb in range(2):
            tmpl.append(("brA", [(0, 512, blk_cols(self.w_br, 0, 8, cb * 512, 512))], 8, 512))
            tmpl.append(("brB", [(0, 512, blk_cols(self.w_br, 8, 8, cb * 512, 512))], 8, 512))
            tmpl.append(("gA", [(0, 512, blk_cols(self.w_gate, 0, 8, cb * 512, 512))], 8, 512))
            tmpl.append(("gB", [(0, 512, blk_cols(self.w_gate, 0, 8, D + cb * 512, 512))], 8, 512))
        for cb in range(2):
            tmpl.append(("out", [(0, 512, blk_cols(self.w_out, 0, 8, cb * 512, 512))], 8, 512))
        tmpl += ffn_blocks(self.w2gu, self.w2d)
        self._wtmpl = tmpl
        self._wn = len(tmpl)
        self._wtotal = self._wn * ntiles_total
        self._w_res = [[Res(), Res()] for _ in range(4)]
        self.wscr = self.dscr("wscr", [self._wn, 128, 4096], BF16)
        self._r_wscr = [Res() for _ in range(self._wn)]

    def _issue_w(self):
        i = self._w_issued
        b = i % self._wn
        ti = i // self._wn
        kind, parts, nkc, ncols = self._wtmpl[b]
        slot = self.wsl[i % 4]
        res = self._w_res[i % 4]
        view = slot.t[:, 0:nkc * ncols].rearrange("p (kc n) -> p kc n", n=ncols)
        if ti >= 2:
            self.S.dma("pool", lambda e, o=slot.t[:, 0:nkc * ncols], s_=self.wscr[b, :, 0:nkc * ncols]: e.dma_start(out=o, in_=s_),
                       writes=[res[0], self._r_wscr[b]])
        else:
            for pi, (c0, nc_, src) in enumerate(parts):
                self.S.dma("pool", lambda e, o=view[:, :, c0:c0 + nc_], s_=src: e.dma_start(out=o, in_=s_),
                           writes=[res[pi]])
            if ti == 0 and self._wtotal > 2 * self._wn:
                dview = self.wscr[b, :, 0:nkc * ncols].rearrange("p (kc n) -> p kc n", n=ncols)
                for pi, (c0, nc_, src) in enumerate(parts):
                    self.S.dma("pool", lambda e, o=dview[:, :, c0:c0 + nc_], s_=src: e.dma_start(out=o, in_=s_),
                               reads=[self._r_wscr[b]])
        self._w_issued += 1

    def wgroup(self, kinds):
        i0 = self._w_cursor
        out = []
        lim = min(self._wtotal, i0 + 4)
        assert len(kinds) <= 4
        while self._w_issued < lim:
            self._issue_w()
        for n, kind in enumerate(kinds):
            i = i0 + n
            k, parts, nkc, ncols = self._wtmpl[i % self._wn]
            assert k == kind, (k, kind, i)
            slot = self.wsl[i % 4]
            view = slot.t[:, 0:nkc * ncols].rearrange("p (kc n) -> p kc n", n=ncols)
            out.append((view, self._w_res[i % 4]))
        self._w_cursor += len(kinds)
        return out

    def wnext(self, kind):
        return self.wgroup([kind])[0]

    def prologue(self):
        S = self.S

        def ld(q, dst, src, reads=()):
            return S.dma(q, lambda e, o=dst.t[:], s=src: e.dma_start(out=o, in_=s), reads=reads, writes=dst.r)

        ld("sp", self.ident_f, self.c_ident)
        ld("pool", self.ident_b, self.c_ident)
        ld("sp", self.biasH, self.c_biasH)
        S.dma("pool", lambda e: e.dma_start(out=self.cmask.t[:].rearrange("p h n -> p (h n)"), in_=self.c_cmask),
              writes=self.cmask.r)
        ld("sp", self.trilT, self.c_tril)
        S.dma("sp", lambda e: e.dma_start(out=self.bsb.t[:].rearrange("p g n -> p (g n)"), in_=self.bs_b),
              writes=self.bsb.r)
        ld("sp", self.vngb, self.vng_b)
        ld("sp", self.gq8, self.gq_b)
        ld("sp", self.gkb, self.gk_b)
        ld("sp", self.lqb, self.lq_b)
        S.dma("sp", lambda e: e.dma_start(out=self.ng.t[:].rearrange("p i c -> p (i c)"), in_=self.ngT),
              writes=self.ng.r)
        ld("sp", self.bgate, self.bgT)
        ld("sp", self.subgs, self.sublnT)
        ld("sp", self.adab, self.ada_bT)
        ld("sp", self.cvt, self.cvecT)
        S.dma("pool", lambda e: e.dma_start(out=self.QT.t[64:66, :, :].rearrange("p a n -> p (a n)"), in_=self.c_qaug),
              writes=[self.r_qaug])
        S.dve(lambda e: e.memset(self.ones_b.t[:], 1.0), writes=self.ones_b.r)
        S.dve(lambda e: e.memset(self.epsT.t[:], EPS), writes=self.epsT.r)
        for i in range(4):
            S.dve(lambda e, i=i: e.memset(self.slotK[i].t[64:66, :, :], 1.0), writes=[self.r_slot_ones[i]])
        S.dve(lambda e: e.tensor_single_scalar(out=self.gq8.t[:], in_=self.gq8.t[:], scalar=0.125, op=ALU.mult),
              reads=self.gq8.r, writes=self.gq8.r)
        S.dve(lambda e: e.tensor_single_scalar(out=self.subgs.t[:], in_=self.subgs.t[:], scalar=1.0 - LAM0, op=ALU.mult),
              reads=self.subgs.r, writes=self.subgs.r)
        lt = self.lam_t
        tmp = self.scr()
        S.dve(lambda e: e.tensor_tensor(out=tmp.t[:, 0:64], in0=self.lqb.t[:, 0:64], in1=self.lqb.t[:, 64:128], op=ALU.mult),
              reads=self.lqb.r, writes=tmp.r)
        S.dve(lambda e: e.tensor_tensor(out=tmp.t[:, 64:128], in0=self.lqb.t[:, 128:192], in1=self.lqb.t[:, 192:256],
                                        op=ALU.mult), reads=self.lqb.r + tmp.r, writes=tmp.r)
        S.dve(lambda e: e.tensor_reduce(out=lt.t[:, 0:2], in_=tmp.t[:, 0:128].rearrange("p (a d) -> p a d", d=64),
                                        axis=AX.X, op=ALU.add), reads=tmp.r, writes=lt.r)
        S.act(lambda e: e.activation(out=lt.t[:, 2:4], in_=lt.t[:, 0:2], func=AF.Exp), reads=lt.r, writes=lt.r)
        S.dve(lambda e: e.tensor_tensor(out=lt.t[:, 4:5], in0=lt.t[:, 3:4], in1=lt.t[:, 2:3], op=ALU.subtract),
              reads=lt.r, writes=lt.r)
        S.dve(lambda e: e.tensor_single_scalar(out=self.neg_lam.t[:], in_=lt.t[:, 4:5], scalar=-LAM0, op=ALU.add),
              reads=lt.r, writes=self.neg_lam.r)
        wsf_view = self.big.t[:, 0:4, :].rearrange("p a n -> p (a n)").bitcast(F32)
        wsf = wsf_view.rearrange("p (g s) -> p g s", s=128)
        rbig = self.big.r[0:4]
        S.dma("sp", lambda e: e.dma_start(out=wsf, in_=self.ws.rearrange("g t s -> t g s")), writes=rbig)
        for half in range(2):
            bk = self.bank()
            for gg in range(4):
                g = half * 4 + gg
                S.pe(lambda e, g=g, gg=gg, bk=bk: e.transpose(out=bk.t[:, gg * 128:(gg + 1) * 128], in_=wsf[:, g, :],
                                                              identity=self.ident_f.t[:]),
                     reads=rbig + self.ident_f.r, writes=bk.r)
            S.dve(lambda e, half=half, bk=bk: e.tensor_tensor(
                out=self.wsT.t[:, half * 4:(half + 1) * 4, :],
                in0=bk.t[:].rearrange("p (g t) -> p g t", t=128),
                in1=self.trilT.t[:].rearrange("p (o t) -> p o t", o=1).to_broadcast([128, 4, 128]),
                op=ALU.mult), reads=bk.r + self.trilT.r, writes=self.wsT.r)
        S.act(lambda e: e.activation(out=self.cs.t[:], in_=self.cvt.t[:], func=AF.Silu), reads=self.cvt.r, writes=self.cs.r)
        mbank = self.pb[7]
        self._reserved = {7}
        rhalf = [self.xT.r[0:4], self.xT.r[4:8]]
        nblk = 9 * D // 256
        for b in range(nblk):
            hb = b % 2
            slot = self.xT.t[:, hb * 4:(hb + 1) * 4, :].rearrange("p a (b n) -> p (a b) n", n=256)
            src = self.ada_w[:, b * 256:(b + 1) * 256].rearrange("(kc p) n -> p kc n", p=128)
            S.dma("sp", lambda e, o=slot, s=src: e.dma_start(out=o, in_=s), writes=rhalf[hb])
            bk = self.bank()
            for c in range(8):
                S.pe(lambda e, c=c, bk=bk, slot=slot: e.matmul(bk.t[0:2, 0:256], lhsT=self.cs.t[:, 2 * c:2 * c + 2],
                                                               rhs=slot[:, c, :], start=(c == 0), stop=(c == 7)),
                     reads=self.cs.r + rhalf[hb], writes=bk.r)
            mrow = self.scr()
            S.act(lambda e, bk=bk, mrow=mrow: e.copy(out=mrow.t[0:2, 0:256], in_=bk.t[0:2, 0:256]), reads=bk.r, writes=mrow.r)
            for k in range(2):
                j = 2 * b + k
                S.pe(lambda e, k=k, j=j, mrow=mrow: e.transpose(out=mbank.t[:, 2 * j:2 * j + 2],
                                                                in_=mrow.t[0:2, k * 128:(k + 1) * 128],
                                                                identity=self.ident_f.t[0:2, 0:2]),
                     reads=mrow.r + self.ident_f.r, writes=mbank.r)
        S.dve(lambda e: e.tensor_tensor(out=self.modT.t[:], in0=mbank.t[:, 0:144].rearrange("p (j s) -> p j s", s=2),
                                        in1=self.adab.t[:].rearrange("p (j o) -> p j o", o=1).to_broadcast([128, 72, 2]),
                                        op=ALU.add), reads=mbank.r + self.adab.r, writes=self.modT.r)
        self._reserved = set()
        for s in range(2):
            for n in range(3):
                sc = self.modT.t[:, (3 * n + 1) * 8:(3 * n + 2) * 8, s]
                sh = self.modT.t[:, (3 * n) * 8:(3 * n + 1) * 8, s]
                gt = self.modT.t[:, (3 * n + 2) * 8:(3 * n + 3) * 8, s]
                S.dve(lambda e, s=s, n=n, sc=sc: e.scalar_tensor_tensor(out=self.dv.t[:, s, n, :], in0=sc, scalar=1.0,
                                                                        in1=self.ng.t[:, n, :], op0=ALU.add, op1=ALU.mult),
                      reads=self.modT.r + self.ng.r, writes=self.dv.r)
                S.dve(lambda e, s=s, n=n, sh=sh: e.tensor_copy(out=self.dv.t[:, s, 3 + n, :], in_=sh),
                      reads=self.modT.r, writes=self.dv.r)
                S.dve(lambda e, s=s, n=n, gt=gt: e.tensor_single_scalar(out=self.dv.t[:, s, 6 + n, :], in_=gt,
                                                                        scalar=(1.0 if n == 1 else 0.5), op=ALU.mult),
                      reads=self.modT.r, writes=self.dv.r)
            S.dve(lambda e, s=s: e.tensor_copy(out=self.dv.t[:, s, 9, :], in_=self.ng.t[:, 3, :]),
                  reads=self.ng.r, writes=self.dv.r)

    def dvs(self, s, k, c):
        return self.dv.t[:, s, k, c:c + 1]

    def rms_stats(self, NT, src_fn, src_res_fn, nchunks, inv_n):
        S = self.S
        bk = self.bank()
        for c in range(nchunks):
            sq = self.rot("sqc")
            S.act(lambda e, c=c, sq=sq: e.activation(out=sq.t[:, 0:NT], in_=src_fn(c), func=AF.Square),
                  reads=src_res_fn(c), writes=sq.r)
            S.pe(lambda e, c=c, sq=sq: e.matmul(bk.t[:, 0:NT], lhsT=self.ones_b.t[:], rhs=sq.t[:, 0:NT],
                                                start=(c == 0), stop=(c == nchunks - 1)),
                 reads=sq.r + self.ones_b.r, writes=bk.r)
        rt = self.scr()
        S.act(lambda e: e.activation(out=rt.t[:, 0:NT], in_=bk.t[:, 0:NT], func=AF.Ln, bias=self.epsT.t[:], scale=inv_n),
              reads=bk.r + self.epsT.r, writes=rt.r)
        rstd = self.rot("rstdb")
        S.act(lambda e: e.activation(out=rstd.t[:, 0:NT], in_=rt.t[:, 0:NT], func=AF.Exp, scale=-0.5), reads=rt.r, writes=rstd.r)
        return rstd

    def rmsnorm_mod(self, NT, s, n):
        S = self.S
        rstd = self.rms_stats(NT, lambda c: self.xT.t[:, c, 0:NT], lambda c: [self.xT.r[c]], 8, 1.0 / D)
        for c in range(8):
            t = self.scr()
            S.dve(lambda e, c=c, t=t: e.tensor_tensor(out=t.t[:, 0:NT], in0=self.xT.t[:, c, 0:NT], in1=rstd.t[:, 0:NT],
                                                      op=ALU.mult), reads=[self.xT.r[c]] + rstd.r, writes=t.r)
            S.act(lambda e, c=c, t=t: e.activation(out=self.hT.t[:, c, 0:NT], in_=t.t[:, 0:NT], func=AF.Identity,
                                                   bias=self.dvs(s, 3 + n, c), scale=self.dvs(s, n, c)),
                  reads=t.r + self.dv.r, writes=[self.hT.r[c]])

    def ffn(self, NT, s, gt_k):
        S = self.S
        hid = self.big
        for p in range(NJ // 2):
            w, wr = self.wnext("gu")
            for jj in range(2):
                j = 2 * p + jj
                bg = self.bank()
                for c in range(8):
                    S.pe(lambda e, c=c, bg=bg, w=w, jj=jj: e.matmul(bg.t[:, 0:NT], lhsT=w[:, c, jj * 128:(jj + 1) * 128],
                                                                     rhs=self.hT.t[:, c, 0:NT], start=(c == 0), stop=(c == 7)),
                         reads=wr + [self.hT.r[c]], writes=bg.r)
                bu = self.bank()
                for c in range(8):
                    S.pe(lambda e, c=c, bu=bu, w=w, jj=jj: e.matmul(bu.t[:, 0:NT],
                                                                     lhsT=w[:, c, 256 + jj * 128:256 + (jj + 1) * 128],
                                                                     rhs=self.hT.t[:, c, 0:NT], start=(c == 0), stop=(c == 7)),
                         reads=wr + [self.hT.r[c]], writes=bu.r)
                sg = self.scr()
                S.act(lambda e, bg=bg, sg=sg: e.activation(out=sg.t[:, 0:NT], in_=bg.t[:, 0:NT], func=AF.Silu),
                      reads=bg.r, writes=sg.r)
                S.dve(lambda e, bu=bu, sg=sg, j=j: e.tensor_tensor(out=hid.t[:, j, 0:NT], in0=bu.t[:, 0:NT], in1=sg.t[:, 0:NT],
                                                                   op=ALU.mult), reads=bu.r + sg.r, writes=[hid.r[j]])
        for cp in range(4):
            (w0, wr0), (w1, wr1) = self.wgroup(["d", "d"])
            for mm in range(2):
                mc = 2 * cp + mm
                bk = self.bank()
                for kc in range(NJ):
                    w, wr = (w0, wr0) if kc < 11 else (w1, wr1)
                    S.pe(lambda e, kc=kc, bk=bk, w=w, mm=mm: e.matmul(bk.t[:, 0:NT], lhsT=w[:, kc % 11, mm * 128:(mm + 1) * 128],
                                                                       rhs=hid.t[:, kc, 0:NT], start=(kc == 0),
                                                                       stop=(kc == NJ - 1)),
                         reads=wr + [hid.r[kc]], writes=bk.r)
                S.dve(lambda e, bk=bk, mc=mc: e.scalar_tensor_tensor(out=self.xT.t[:, mc, 0:NT], in0=bk.t[:, 0:NT],
                                                                     scalar=self.dvs(s, gt_k, mc), in1=self.xT.t[:, mc, 0:NT],
                                                                     op0=ALU.mult, op1=ALU.add),
                      reads=bk.r + self.dv.r + [self.xT.r[mc]], writes=[self.xT.r[mc]])

    def prefetch_x(self, x_dram, row0, TS, nsub):
        S = self.S
        lst = []
        for i in range(min(2, nsub)):
            xi = self.rot("xin")
            S.dma("sp", lambda e, xi=xi, i=i: e.dma_start(out=xi.t[0:TS, :], in_=x_dram[row0 + i * TS:row0 + (i + 1) * TS, :]),
                  writes=xi.r)
            lst.append(xi)
        self._x_pref = (row0, id(x_dram), lst)

    def load_x(self, x_dram, row0, NT, TS, nsub):
        S = self.S
        pref = getattr(self, "_x_pref", None)
        if pref is not None and (pref[0], pref[1]) != (row0, id(x_dram)):
            pref = None
        self._x_pref = None
        for g0 in range(0, nsub, 2):
            subs = list(range(g0, min(nsub, g0 + 2)))
            xs_ = []
            for n, i in enumerate(subs):
                if g0 == 0 and pref is not None:
                    xs_.append(pref[2][n])
                    continue
                xi = self.rot("xin")
                S.dma("sp", lambda e, xi=xi, i=i: e.dma_start(out=xi.t[0:TS, :], in_=x_dram[row0 + i * TS:row0 + (i + 1) * TS, :]),
                      writes=xi.r)
                xs_.append(xi)
            for c in range(8):
                bk = self.bank()
                for n, i in enumerate(subs):
                    S.pe(lambda e, c=c, n=n, bk=bk, xi=xs_[n]: e.transpose(out=bk.t[:, n * TS:(n + 1) * TS],
                                                                          in_=xi.t[0:TS, c * 128:(c + 1) * 128],
                                                                          identity=self.ident_f.t[0:TS, 0:TS]),
                         reads=xs_[n].r + self.ident_f.r, writes=bk.r)
                w = len(subs) * TS
                if self.alt():
                    S.act(lambda e, c=c, bk=bk, w=w, g0=g0: e.copy(out=self.xT.t[:, c, g0 * TS:g0 * TS + w], in_=bk.t[:, 0:w]),
                          reads=bk.r, writes=[self.xT.r[c]])
                else:
                    S.dve(lambda e, c=c, bk=bk, w=w, g0=g0: e.tensor_copy(out=self.xT.t[:, c, g0 * TS:g0 * TS + w], in_=bk.t[:, 0:w]),
                          reads=bk.r, writes=[self.xT.r[c]])

    def store_y(self, y_dram, row0, NT, TS, nsub, s):
        S = self.S
        rstd = self.rms_stats(NT, lambda c: self.xT.t[:, c, 0:NT], lambda c: [self.xT.r[c]], 8, 1.0 / D)
        for c in range(8):
            S.dve(lambda e, c=c: e.scalar_tensor_tensor(out=self.xT.t[:, c, 0:NT], in0=self.xT.t[:, c, 0:NT],
                                                        scalar=self.dvs(s, 9, c), in1=rstd.t[:, 0:NT],
                                                        op0=ALU.mult, op1=ALU.mult),
                  reads=[self.xT.r[c]] + rstd.r + self.dv.r, writes=[self.xT.r[c]])
        for i in range(nsub):
            yo = self.rot("yout")
            for cq in range(2):
                bk = self.bank()
                for k in range(4):
                    c = cq * 4 + k
                    S.pe(lambda e, c=c, k=k, bk=bk, i=i: e.transpose(out=bk.t[0:TS, k * 128:(k + 1) * 128],
                                                                     in_=self.xT.t[:, c, i * TS:(i + 1) * TS],
                                                                     identity=self.ident_f.t[:]),
                         reads=[self.xT.r[c]] + self.ident_f.r, writes=bk.r)
                if self.alt():
                    S.act(lambda e, bk=bk, yo=yo, cq=cq: e.copy(out=yo.t[0:TS, cq * 512:(cq + 1) * 512], in_=bk.t[0:TS, :]),
                          reads=bk.r, writes=yo.r)
                else:
                    S.dve(lambda e, bk=bk, yo=yo, cq=cq: e.tensor_copy(out=yo.t[0:TS, cq * 512:(cq + 1) * 512], in_=bk.t[0:TS, :]),
                          reads=bk.r, writes=yo.r)
            self.fin.append(S.dma("sp", lambda e, yo=yo, i=i: e.dma_start(out=y_dram[row0 + i * TS:row0 + (i + 1) * TS, :],
                                                                          in_=yo.t[0:TS, :]), reads=yo.r))

    def kT_store(self, src_b, TS, hm0, ktscr, key0, rscr):
        S = self.S
        bk = self.bank()
        bkb = bk.t[:].bitcast(BF16)
        for a in range(8):
            S.pe(lambda e, a=a: e.transpose(out=bkb[0:64, a * 128:a * 128 + TS], in_=src_b.t[0:TS, a * 64:(a + 1) * 64],
                                            identity=self.ident_b.t[0:TS, 0:TS]),
                 reads=src_b.r + self.ident_b.r, writes=bk.r)
        stg = self.rot("KTst")
        S.act(lambda e: e.copy(out=stg.t[:, :, 0:TS], in_=bkb[0:64, :].rearrange("p (a n) -> p a n", n=128)[:, :, 0:TS]),
              reads=bk.r, writes=stg.r)
        dst = ktscr.rearrange("h m d k -> d (h m) k")[:, hm0:hm0 + 8, key0:key0 + TS]
        S.dma("sp", lambda e: e.dma_start(out=dst, in_=stg.t[:, :, 0:TS]), reads=stg.r + [rscr[0]])

    def qkv(self, NT, TS, nsub, k_dram, v_dram, row0, ktscr, vscr, key0, blk0, rscr):
        S = self.S
        pending = []
        prev_b = [None]

        def tick(force=False):
            for ent in list(pending):
                ent[0] -= 1
                if ent[0] <= 0 or force:
                    ent[1]()
                    pending.remove(ent)

        for cg in range(6):
            w, wr = self.wnext("in")
            kind = "qkv"[cg // 2]
            half = cg % 2
            for i in range(nsub):
                bk = self.bank()
                for c in range(8):
                    S.pe(lambda e, c=c, bk=bk, w=w, i=i: e.matmul(bk.t[0:TS, :], lhsT=self.hT.t[:, c, i * TS:(i + 1) * TS],
                                                                  rhs=w[:, c, :], start=(c == 0), stop=(c == 7)),
                         reads=wr + [self.hT.r[c]], writes=bk.r)
                tick()
                if kind == "v":
                    vf = self.rot("vf")
                    vb = self.rot("vst")
                    S.act(lambda e, bk=bk, vf=vf: e.copy(out=vf.t[0:TS, :], in_=bk.t[0:TS, :]), reads=bk.r, writes=vf.r)
                    S.dve(lambda e, vf=vf, vb=vb: e.tensor_copy(out=vb.t[0:TS, :], in_=vf.t[0:TS, :]), reads=vf.r, writes=vb.r)
                    self.fin.append(S.dma("sp", lambda e, vf=vf, i=i, half=half: e.dma_start(
                        out=v_dram[row0 + i * TS:row0 + (i + 1) * TS, half * 512:(half + 1) * 512], in_=vf.t[0:TS, :]),
                        reads=vf.r))
                    S.dma("sp", lambda e, vb=vb, i=i, half=half: e.dma_start(
                        out=vscr[0:TS, blk0 + i, half * 512:(half + 1) * 512], in_=vb.t[0:TS, :]), reads=vb.r + [rscr[1]])
                    if prev_b[0] is not None:
                        prev_b[0]()
                        prev_b[0] = None
                    continue
                sq = self.scr()
                S.act(lambda e, bk=bk, sq=sq: e.activation(out=sq.t[0:TS, :], in_=bk.t[0:TS, :], func=AF.Square),
                      reads=bk.r, writes=sq.r)
                ss = self.st8n()
                S.dve(lambda e, sq=sq, ss=ss: e.tensor_reduce(out=ss.t[0:TS, :], in_=sq.t[0:TS, :].rearrange("p (g d) -> p g d", d=64),
                                                              axis=AX.X, op=ALU.add), reads=sq.r, writes=ss.r)

                def stage_b(bk=bk, ss=ss, kind=kind, half=half, i=i):
                    S.act(lambda e: e.activation(out=ss.t[0:TS, :], in_=ss.t[0:TS, :], func=AF.Sqrt, bias=self.epsT.t[0:TS, :],
                                                 scale=1.0 / 64), reads=ss.r + self.epsT.r, writes=ss.r)
                    S.dve(lambda e: e.reciprocal(out=ss.t[0:TS, :], in_=ss.t[0:TS, :]), reads=ss.r, writes=ss.r)
                    t = self.scr()
                    S.dve(lambda e: e.tensor_tensor(
                        out=t.t[0:TS, :].rearrange("p (g d) -> p g d", d=64),
                        in0=bk.t[0:TS, :].rearrange("p (g d) -> p g d", d=64),
                        in1=ss.t[0:TS, :].rearrange("p (g o) -> p g o", o=1).to_broadcast([TS, 8, 64]),
                        op=ALU.mult), reads=bk.r + ss.r, writes=t.r)
                    if kind == "k":
                        kf = self.rot("knf")
                        kb = self.rot("knb")
                        S.dve(lambda e: e.tensor_tensor(
                            out=kf.t[0:TS, :].rearrange("p (g d) -> p g d", d=64),
                            in0=t.t[0:TS, :].rearrange("p (g d) -> p g d", d=64),
                            in1=self.gkb.t[0:TS, :].rearrange("p (o d) -> p o d", o=1).to_broadcast([TS, 8, 64]),
                            op=ALU.mult), reads=t.r + self.gkb.r, writes=kf.r)
                        S.act(lambda e: e.copy(out=kb.t[0:TS, :], in_=kf.t[0:TS, :]), reads=kf.r, writes=kb.r)
                        self.fin.append(S.dma("sp", lambda e: e.dma_start(
                            out=k_dram[row0 + i * TS:row0 + (i + 1) * TS, half * 512:(half + 1) * 512], in_=kf.t[0:TS, :]),
                            reads=kf.r))
                        pending.append([3, lambda: self.kT_store(kb, TS, half * 8, ktscr, key0 + i * TS, rscr)])
                    else:
                        qb = self.rot("qnb")
                        S.dve(lambda e: e.tensor_tensor(
                            out=qb.t[0:TS, :].rearrange("p (g d) -> p g d", d=64),
                            in0=t.t[0:TS, :].rearrange("p (g d) -> p g d", d=64),
                            in1=self.gq8.t[0:TS, :].rearrange("p (o d) -> p o d", o=1).to_broadcast([TS, 8, 64]),
                            op=ALU.mult), reads=t.r + self.gq8.r, writes=qb.r)

                        def q_tr():
                            bq = self.bank()
                            bqb = bq.t[:].bitcast(BF16)
                            for a in range(8):
                                S.pe(lambda e, a=a: e.transpose(out=bqb[0:64, a * 128:a * 128 + TS],
                                                                in_=qb.t[0:TS, a * 64:(a + 1) * 64],
                                                                identity=self.ident_b.t[0:TS, 0:TS]),
                                     reads=qb.r + self.ident_b.r, writes=bq.r)
                            S.act(lambda e: e.copy(
                                out=self.QT.t[0:64, half * 8:(half + 1) * 8, i * TS:(i + 1) * TS],
                                in_=bqb[0:64, :].rearrange("p (a n) -> p a n", n=128)[:, :, 0:TS]),
                                reads=bq.r, writes=self.QT.r[half * 8:(half + 1) * 8])
                        pending.append([3, q_tr])

                if prev_b[0] is not None:
                    prev_b[0]()
                prev_b[0] = stage_b
        if prev_b[0] is not None:
            prev_b[0]()
            prev_b[0] = None
        tick(force=True)

    def attention(self, NT, chunks_by_head, ktscr, vscr):
        S = self.S
        accO = [self.pb[0], self.pb[1]]
        accS = [self.pb[2], self.pb[3]]
        valid = []
        loads = []
        for h in range(H):
            for ci, ch in enumerate(chunks_by_head[h]):
                loads.append((h, ci))
                nb = (ch["nk"] + 127) // 128
                for j in range(nb):
                    if ch["diag"] and 128 * j >= NT:
                        continue
                    valid.append((h, ci, j))
        load_idx = {k: n for n, k in enumerate(loads)}
        slot_of = {}
        state = {"issued": 0}

        def issue_load():
            n = state["issued"]
            h, ci = loads[n]
            ch = chunks_by_head[h][ci]
            sl = n % 4
            slot_of[(h, ci)] = sl
            nk = ch["nk"]
            nb = (nk + 127) // 128
            srcK = ktscr[h].rearrange("m d k -> d m k")[:, :, ch["key0"]:ch["key0"] + nk]
            S.dma("sp", lambda e, sl=sl, srcK=srcK, nk=nk: e.dma_start(out=self.slotK[sl].t[0:64, :, 0:nk], in_=srcK),
                  writes=[ch["rscr"][0], self.r_slotK[sl]])
            kl = min(128, nk)
            srcV = vscr[0:kl, ch["blk0"]:ch["blk0"] + nb, h * 128:(h + 1) * 128]
            S.dma("sp", lambda e, sl=sl, srcV=srcV, kl=kl, nb=nb: e.dma_start(out=self.slotV[sl].t[0:kl, 0:nb, :], in_=srcV),
                  writes=[ch["rscr"][1], self.r_slotV[sl]])
            state["issued"] = n + 1

        def emit_S(h, ci, j):
            ch = chunks_by_head[h][ci]
            idx = load_idx[(h, ci)]
            while state["issued"] < min(len(loads), idx + 3):
                issue_load()
            sl = slot_of[(h, ci)]
            nkb = min(128, ch["nk"] - 128 * j)
            qlo = 128 * j if ch["diag"] else 0
            dist = ch["dist0"] - j
            spair = self.SP[self._sb_i % 2]
            self._sb_i += 1
            for m in range(2):
                sbk_t = spair.t[:, m, :]
                sbk_r = [spair.r[m]]
                rq = [self.QT.r[2 * h + m], self.r_qaug, self.r_slotK[sl], self.r_slot_ones[sl]]
                S.pe(lambda e, m=m, sbk_t=sbk_t, sl=sl, nkb=nkb, qlo=qlo, j=j, h=h, diag=ch["diag"]: e.matmul(
                    sbk_t[0:nkb, qlo:NT], lhsT=self.slotK[sl].t[0:66, m, j * 128:j * 128 + nkb],
                    rhs=self.QT.t[0:66, 2 * h + m, qlo:NT], start=True, stop=(not diag)),
                    reads=rq, writes=sbk_r)
                if ch["diag"]:
                    cw = min(128, NT - qlo)
                    S.pe(lambda e, sbk_t=sbk_t, nkb=nkb, qlo=qlo, cw=cw, h=h: e.matmul(
                        sbk_t[0:nkb, qlo:qlo + cw], lhsT=self.ident_b.t[0:nkb, 0:nkb], rhs=self.cmask.t[0:nkb, h, 0:cw],
                        start=False, stop=True), reads=self.ident_b.r + self.cmask.r, writes=sbk_r)
            pt = self.PT[self._pt_i % 2]
            self._pt_i += 1
            col = h * NDIST + dist + 3
            S.act(lambda e, spair=spair, pt=pt, nkb=nkb, qlo=qlo, col=col: e.activation(
                out=pt.t[0:nkb, :, qlo:NT], in_=spair.t[0:nkb, :, qlo:NT], func=AF.Exp, bias=self.biasH.t[0:nkb, col:col + 1],
                scale=1.0), reads=spair.r + self.biasH.r, writes=pt.r)
            pts = pt
            return (h, ci, j, sl, nkb, qlo, pts)

        def emit_AV(item, first, last):
            h, ci, j, sl, nkb, qlo, pt = item
            for m in range(2):
                S.pe(lambda e, m=m, sl=sl, nkb=nkb, qlo=qlo, j=j, pt=pt: e.matmul(
                    accO[m].t[:, qlo:NT], lhsT=self.slotV[sl].t[0:nkb, j, :], rhs=pt.t[0:nkb, m, qlo:NT],
                    start=first, stop=last), reads=[self.r_slotV[sl]] + pt.r, writes=accO[m].r)
                S.pe(lambda e, m=m, nkb=nkb, qlo=qlo, pt=pt: e.matmul(
                    accS[m].t[:, qlo:NT], lhsT=self.ones_b.t[0:nkb, :], rhs=pt.t[0:nkb, m, qlo:NT],
                    start=first, stop=last), reads=self.ones_b.r + pt.r, writes=accS[m].r)

        def epilogue_a(h, final=False):
            cO = [self.scr(), self.scr()]
            cS = [self.scr(), self.scr()]
            S.dve(lambda e: e.tensor_copy(out=cO[0].t[:, 0:NT], in_=accO[0].t[:, 0:NT]), reads=accO[0].r, writes=cO[0].r)
            S.act(lambda e: e.copy(out=cO[1].t[:, 0:NT], in_=accO[1].t[:, 0:NT]), reads=accO[1].r, writes=cO[1].r)
            S.dve(lambda e: e.tensor_copy(out=cS[0].t[:, 0:NT], in_=accS[0].t[:, 0:NT]), reads=accS[0].r, writes=cS[0].r)
            S.act(lambda e: e.copy(out=cS[1].t[:, 0:NT], in_=accS[1].t[:, 0:NT]), reads=accS[1].r, writes=cS[1].r)
            for m in range(2):
                S.dve(lambda e, m=m: e.reciprocal(out=cS[m].t[:, 0:NT], in_=cS[m].t[:, 0:NT]), reads=cS[m].r, writes=cS[m].r)
                S.dve(lambda e, m=m: e.tensor_tensor(out=cO[m].t[:, 0:NT], in0=cO[m].t[:, 0:NT], in1=cS[m].t[:, 0:NT],
                                                     op=ALU.mult), reads=cO[m].r + cS[m].r, writes=cO[m].r)
            od_t = self.gzf.t[:, (h % 2) * 512:(h % 2) * 512 + 512]
            od_r = [self._r_od[h % 2]]
            if final:
                od_t = self.rstdb[0].t[:, :]
                od_r = self.rstdb[0].r
            osq = self.osq[h % 2]
            S.dve(lambda e: e.scalar_tensor_tensor(out=od_t[:, 0:NT], in0=cO[1].t[:, 0:NT], scalar=self.neg_lam.t[:, 0:1],
                                                   in1=cO[0].t[:, 0:NT], op0=ALU.mult, op1=ALU.add),
                  reads=cO[0].r + cO[1].r + self.neg_lam.r, writes=od_r)
            S.dve(lambda e: e.tensor_tensor(out=osq.t[:, 0:NT], in0=od_t[:, 0:NT], in1=od_t[:, 0:NT], op=ALU.mult),
                  reads=od_r, writes=osq.r)
            return (h, (od_t, od_r, osq))

        def epilogue_b(h, st):
            od_t, od_r, osq = st
            spair = self.SP[self._sb_i % 2]
            self._sb_i += 1
            bn = TB(spair.t[:, 0, :], 0)
            bn.r = [spair.r[0]]
            S.pe(lambda e: e.matmul(bn.t[:, 0:NT], lhsT=self.ones_b.t[:], rhs=osq.t[:, 0:NT], start=True, stop=True),
                 reads=osq.r + self.ones_b.r, writes=bn.r)
            rt = self.scr()
            S.act(lambda e: e.activation(out=rt.t[:, 0:NT], in_=bn.t[:, 0:NT], func=AF.Ln, bias=self.epsT.t[:], scale=1.0 / 128),
                  reads=bn.r + self.epsT.r, writes=rt.r)
            rstd = self.scr()
            S.act(lambda e: e.activation(out=rstd.t[:, 0:NT], in_=rt.t[:, 0:NT], func=AF.Exp, scale=-0.5), reads=rt.r, writes=rstd.r)
            S.dve(lambda e: e.scalar_tensor_tensor(out=self.oT.t[:, h, 0:NT], in0=od_t[:, 0:NT], scalar=self.subgs.t[:, 0:1],
                                                   in1=rstd.t[:, 0:NT], op0=ALU.mult, op1=ALU.mult),
                  reads=od_r + rstd.r + self.subgs.r, writes=[self.oT.r[h]])

        per_head = {}
        for (h, ci, j) in valid:
            per_head.setdefault(h, []).append((ci, j))
        pending_b = []

        def tick_b(force=False):
            for ent in list(pending_b):
                ent[0] -= 1
                if ent[0] <= 0 or force:
                    epilogue_b(ent[1], ent[2])
                    pending_b.remove(ent)

        prev_item = None
        for n, (h, ci, j) in enumerate(valid):
            it = emit_S(h, ci, j)
            if prev_item is not None:
                ph, pci, pj = prev_item[0], prev_item[1], prev_item[2]
                last = (pci, pj) == per_head[ph][-1]
                emit_AV(prev_item, (pci, pj) == per_head[ph][0], last)
                tick_b()
                if last:
                    tick_b(force=True)
                    pending_b.append([10] + list(epilogue_a(ph)))
            prev_item = it
        ph, pci, pj = prev_item[0], prev_item[1], prev_item[2]
        emit_AV(prev_item, (pci, pj) == per_head[ph][0], (pci, pj) == per_head[ph][-1])
        tick_b(force=True)
        last_st = epilogue_a(ph, final=True)
        return lambda: epilogue_b(*last_st)

    def mix_out(self, NT, TS, nsub, s, gs_dram, late=None):
        S = self.S
        sT = lambda g: (self.big.t[:, g, 0:NT], self.big.r[g])
        mixT = lambda c: (self.big.t[:, 8 + c, 0:NT], self.big.r[8 + c])
        uT = lambda c: (self.big.t[:, 16 + c, 0:NT], self.big.r[16 + c])
        gvn_t = self.big.t[:, 24:32, :].rearrange("p (i a) n -> p i (a n)", a=2)
        gvn_r = lambda i: self.big.r[24 + 2 * i:26 + 2 * i]
        (w0, wr0), (w1, wr1) = self.wgroup(["gv", "gv"])
        for i in range(nsub):
            for cg, (w, wr) in enumerate(((w0, wr0), (w1, wr1))):
                bk = self.bank()
                for kc in range(8):
                    S.pe(lambda e, kc=kc, bk=bk, w=w, i=i: e.matmul(bk.t[0:TS, :], lhsT=self.hT.t[:, kc, i * TS:(i + 1) * TS],
                                                                    rhs=w[:, kc, :], start=(kc == 0), stop=(kc == 7)),
                         reads=wr + [self.hT.r[kc]], writes=bk.r)
                S.act(lambda e, bk=bk, cg=cg: e.activation(out=self.gzf.t[0:TS, cg * 512:(cg + 1) * 512], in_=bk.t[0:TS, :],
                                                           func=AF.Gelu_apprx_tanh), reads=bk.r, writes=self.gzf.r)
            ss = self.st8n()
            junk = self.scr()
            for cg in range(2):
                S.act(lambda e, cg=cg, ss=ss, junk=junk: e.activation(out=junk.t[0:TS, :], in_=self.gzf.t[0:TS, cg * 512:(cg + 1) * 512],
                                                                      func=AF.Square, accum_out=ss.t[0:TS, cg:cg + 1]),
                      reads=self.gzf.r, writes=junk.r + ss.r)
            S.dve(lambda e, ss=ss: e.tensor_tensor(out=ss.t[0:TS, 2:3], in0=ss.t[0:TS, 0:1], in1=ss.t[0:TS, 1:2], op=ALU.add),
                  reads=ss.r, writes=ss.r)
            S.act(lambda e, ss=ss: e.activation(out=ss.t[0:TS, 3:4], in_=ss.t[0:TS, 2:3], func=AF.Sqrt, bias=self.epsT.t[0:TS, :],
                                                scale=1.0 / D), reads=ss.r + self.epsT.r, writes=ss.r)
            S.dve(lambda e, ss=ss: e.reciprocal(out=ss.t[0:TS, 4:5], in_=ss.t[0:TS, 3:4]), reads=ss.r, writes=ss.r)
            S.dve(lambda e, ss=ss, i=i: e.scalar_tensor_tensor(out=gvn_t[0:TS, i, :], in0=self.gzf.t[0:TS, :], scalar=ss.t[0:TS, 4:5],
                                                               in1=self.vngb.t[0:TS, :], op0=ALU.mult, op1=ALU.mult),
                  reads=self.gzf.r + ss.r + self.vngb.r, writes=gvn_r(i))
            if gs_dram is not None:
                S.dve(lambda e, ss=ss: e.scalar_tensor_tensor(out=self.gzf.t[0:TS, :], in0=self.gzf.t[0:TS, :], scalar=ss.t[0:TS, 4:5],
                                                              in1=self.vngb.t[0:TS, :], op0=ALU.mult, op1=ALU.mult),
                      reads=self.gzf.r + ss.r + self.vngb.r, writes=self.gzf.r)
                self.fin.append(S.dma("sp", lambda e: e.dma_start(out=gs_dram[0:TS, :], in_=self.gzf.t[0:TS, :]), reads=self.gzf.r))
        for cg in range(2):
            w, wr = self.wnext("u")
            for k in range(4):
                c = cg * 4 + k
                bk = self.bank()
                for kc in range(8):
                    S.pe(lambda e, kc=kc, bk=bk, w=w, k=k: e.matmul(bk.t[:, 0:NT], lhsT=w[:, kc, k * 128:(k + 1) * 128],
                                                                    rhs=self.hT.t[:, kc, 0:NT], start=(kc == 0), stop=(kc == 7)),
                         reads=wr + [self.hT.r[kc]], writes=bk.r)
                S.act(lambda e, bk=bk, c=c: e.activation(out=uT(c)[0], in_=bk.t[:, 0:NT], func=AF.Gelu_apprx_tanh),
                      reads=bk.r, writes=[uT(c)[1]])
        if late is not None:
            late()
        for g in range(8):
            bk = self.bank()
            for i in range(nsub):
                S.pe(lambda e, g=g, i=i, bk=bk: e.matmul(bk.t[:, i * TS:(i + 1) * TS], lhsT=gvn_t[0:TS, i, g * 128:(g + 1) * 128],
                                                         rhs=self.wsT.t[0:TS, g, 0:TS], start=True, stop=True),
                     reads=gvn_r(i) + self.wsT.r, writes=bk.r)
            t = self.scr()
            S.dve(lambda e, g=g, bk=bk, t=t: e.tensor_tensor(
                out=t.t[:, 0:NT].rearrange("p (i n) -> p i n", n=TS),
                in0=bk.t[:, 0:NT].rearrange("p (i n) -> p i n", n=TS),
                in1=self.bsb.t[:, g, 0:TS].rearrange("p (o n) -> p o n", o=1).to_broadcast([128, nsub, TS]),
                op=ALU.add), reads=bk.r + self.bsb.r, writes=t.r)
            S.dve(lambda e, g=g, t=t: e.tensor_tensor(out=sT(g)[0], in0=t.t[:, 0:NT], in1=uT(g)[0], op=ALU.mult),
                  reads=t.r + [uT(g)[1]], writes=[sT(g)[1]])
        for cb in range(2):
            (wA, rA), (wB, rB), (wgA, rgA), (wgB, rgB) = self.wgroup(["brA", "brB", "gA", "gB"])
            for k in range(4):
                mc = cb * 4 + k
                bA = self.bank()
                for kc in range(8):
                    S.pe(lambda e, kc=kc, bA=bA, k=k, wA=wA: e.matmul(bA.t[:, 0:NT], lhsT=wA[:, kc, k * 128:(k + 1) * 128],
                                                                      rhs=self.oT.t[:, kc, 0:NT], start=(kc == 0), stop=(kc == 7)),
                         reads=rA + [self.oT.r[kc]], writes=bA.r)
                bB = self.bank()
                for kc in range(8):
                    S.pe(lambda e, kc=kc, bB=bB, k=k, wB=wB: e.matmul(bB.t[:, 0:NT], lhsT=wB[:, kc, k * 128:(k + 1) * 128],
                                                                      rhs=sT(kc)[0], start=(kc == 0), stop=(kc == 7)),
                         reads=rB + [sT(kc)[1]], writes=bB.r)
                bGA = self.bank()
                for kc in range(8):
                    S.pe(lambda e, kc=kc, bGA=bGA, k=k, wgA=wgA: e.matmul(bGA.t[:, 0:NT], lhsT=wgA[:, kc, k * 128:(k + 1) * 128],
                                                                          rhs=self.hT.t[:, kc, 0:NT], start=(kc == 0), stop=(kc == 7)),
                         reads=rgA + [self.hT.r[kc]], writes=bGA.r)
                bGB = self.bank()
                for kc in range(8):
                    S.pe(lambda e, kc=kc, bGB=bGB, k=k, wgB=wgB: e.matmul(bGB.t[:, 0:NT], lhsT=wgB[:, kc, k * 128:(k + 1) * 128],
                                                                          rhs=self.hT.t[:, kc, 0:NT], start=(kc == 0), stop=(kc == 7)),
                         reads=rgB + [self.hT.r[kc]], writes=bGB.r)
                gA = self.scr()
                S.act(lambda e, bGA=bGA, gA=gA, mc=mc: e.activation(out=gA.t[:, 0:NT], in_=bGA.t[:, 0:NT], func=AF.Sigmoid,
                                                                    bias=self.bgate.t[:, mc:mc + 1], scale=1.0),
                      reads=bGA.r + self.bgate.r, writes=gA.r)
                gB = self.scr()
                S.act(lambda e, bGB=bGB, gB=gB, mc=mc: e.activation(out=gB.t[:, 0:NT], in_=bGB.t[:, 0:NT], func=AF.Sigmoid,
                                                                    bias=self.bgate.t[:, 8 + mc:9 + mc], scale=1.0),
                      reads=bGB.r + self.bgate.r, writes=gB.r)
                S.dve(lambda e, bA=bA, gA=gA: e.tensor_tensor(out=gA.t[:, 0:NT], in0=bA.t[:, 0:NT], in1=gA.t[:, 0:NT], op=ALU.mult),
                      reads=bA.r + gA.r, writes=gA.r)
                S.dve(lambda e, bB=bB, gB=gB: e.tensor_tensor(out=gB.t[:, 0:NT], in0=bB.t[:, 0:NT], in1=gB.t[:, 0:NT], op=ALU.mult),
                      reads=bB.r + gB.r, writes=gB.r)
                S.dve(lambda e, gA=gA, gB=gB, mc=mc: e.tensor_tensor(out=mixT(mc)[0], in0=gA.t[:, 0:NT], in1=gB.t[:, 0:NT], op=ALU.add),
                      reads=gA.r + gB.r, writes=[mixT(mc)[1]])
        for cb in range(2):
            w, wr = self.wnext("out")
            for k in range(4):
                mc = cb * 4 + k
                bk = self.bank()
                for kc in range(8):
                    S.pe(lambda e, kc=kc, bk=bk, w=w, k=k: e.matmul(bk.t[:, 0:NT], lhsT=w[:, kc, k * 128:(k + 1) * 128],
                                                                    rhs=mixT(kc)[0], start=(kc == 0), stop=(kc == 7)),
                         reads=wr + [mixT(kc)[1]], writes=bk.r)
                S.dve(lambda e, bk=bk, mc=mc: e.scalar_tensor_tensor(out=self.xT.t[:, mc, 0:NT], in0=bk.t[:, 0:NT],
                                                                     scalar=self.dvs(s, 7, mc), in1=self.xT.t[:, mc, 0:NT],
                                                                     op0=ALU.mult, op1=ALU.add),
                      reads=bk.r + self.dv.r + [self.xT.r[mc]], writes=[self.xT.r[mc]])

    def tile(self, kind, t=0):
        stage = getattr(self, "stage", 99)
        if kind == "prompt":
            NT, TS, nsub, s = 512, 128, 4, 0
            x_dram, y_dram, k_dram, v_dram, gs_dram = self.xp, self.yp, self.kp, self.vp, None
            row0 = 512 * t
            ktscr, vscr = self.ktscr_p, self.vscr_p
            key0, blk0, rscr = 512 * t, 4 * t, self.r_scr_p[t]
            q0 = 512 * t
            chunks = [dict(key0=512 * c, nk=512, blk0=4 * c, rscr=self.r_scr_p[c], diag=(c == t), dist0=4 * (t - c))
                      for c in range(t + 1)]
        else:
            NT, TS, nsub, s = 16, 16, 1, 1
            x_dram, y_dram, k_dram, v_dram, gs_dram = self.xs, self.ys, self.ks, self.vs, self.gs
            row0 = 0
            ktscr, vscr = self.ktscr_s, self.vscr_s
            key0, blk0, rscr = 1024, 8, self.r_scr_s[2]
            q0 = 1024
            chunks = [dict(key0=0, nk=512, blk0=0, rscr=self.r_scr_s[0], diag=False, dist0=8),
                      dict(key0=512, nk=512, blk0=4, rscr=self.r_scr_s[1], diag=False, dist0=4),
                      dict(key0=1024, nk=16, blk0=8, rscr=self.r_scr_s[2], diag=True, dist0=0)]
        chunks_by_head = []
        for h in range(H):
            slope = 2.0 ** (-(h + 1))
            lst = [ch for ch in chunks
                   if ch["diag"] or slope * (q0 - (ch["key0"] + ch["nk"] - 1)) < ALIBI_SKIP]
            chunks_by_head.append(lst)
        dbg = getattr(self, "debug", False) and kind == getattr(self, "debug_kind", "sample") and t == 0
        self.load_x(x_dram, row0, NT, TS, nsub)
        if dbg:
            self.dump("modT", self.modT.t[:], [128, 72, 2], self.modT.r)
            self.dump("dv", self.dv.t[:], [128, 2, 10, 8], self.dv.r)
            self.dump("xT0", self.xT.t[:, :, 0:NT], [128, 8, NT], self.xT.r)
        if stage < 3:
            return
        self.rmsnorm_mod(NT, s, 0)
        if dbg:
            self.dump("h1", self.hT.t[:, :, 0:NT], [128, 8, NT], self.hT.r, BF16)
        if stage < 4:
            return
        self.ffn(NT, s, 6)
        if dbg:
            self.dump("hid", self.big.t[:, 0:22, 0:NT], [128, 22, NT], self.big.r[0:22], BF16)
            self.dump("xT1", self.xT.t[:, :, 0:NT], [128, 8, NT], self.xT.r)
        self.rmsnorm_mod(NT, s, 1)
        if dbg:
            self.dump("h2", self.hT.t[:, :, 0:NT], [128, 8, NT], self.hT.r, BF16)
        if stage < 5:
            return
        self.qkv(NT, TS, nsub, k_dram, v_dram, row0, ktscr, vscr, key0, blk0, rscr)
        if stage < 6:
            return
        if dbg:
            self.dump("QT", self.QT.t[:, :, 0:NT], [66, 16, NT], self.QT.r + [self.r_qaug], BF16)
        late = self.attention(NT, chunks_by_head, ktscr, vscr)
        nxt = getattr(self, "_next_tile", None)
        if nxt is not None:
            self.prefetch_x(self.xp, 512 * nxt, 128, 4)
        if dbg:
            self.dump("oT", self.oT.t[:, :, 0:NT], [128, 8, NT], self.oT.r, BF16)
        if stage < 7:
            return
        self.mix_out(NT, TS, nsub, s, gs_dram, late)
        if dbg:
            self.dump("sT", self.big.t[:, 0:8, 0:NT], [128, 8, NT], self.big.r[0:8], BF16)
            self.dump("mixT", self.big.t[:, 8:16, 0:NT], [128, 8, NT], self.big.r[8:16], BF16)
            self.dump("xT2", self.xT.t[:, :, 0:NT], [128, 8, NT], self.xT.r)
        self.rmsnorm_mod(NT, s, 2)
        self.ffn(NT, s, 8)
        self.store_y(y_dram, row0, NT, TS, nsub, s)

    def sample_cache_prep(self):
        S = self.S
        for half in range(2):
            S.dma("pool", lambda e, half=half: e.dma_start(
                out=self.vscr_s[:, 4 * half:4 * half + 4, :],
                in_=self.cv[512 * half:512 * half + 512, :].rearrange("(blk kl) n -> kl blk n", kl=128)),
                reads=[self.r_scr_s[half][1]])
        for blk in range(8):
            for half in range(2):
                kf = self.rot("knf")
                kb = self.rot("knb")
                S.dma("sp", lambda e, kf=kf, blk=blk, half=half: e.dma_start(
                    out=kf.t[:, :], in_=self.ck[blk * 128:(blk + 1) * 128, half * 512:(half + 1) * 512]), writes=kf.r)
                S.dve(lambda e, kf=kf, kb=kb: e.tensor_copy(out=kb.t[:, :], in_=kf.t[:, :]), reads=kf.r, writes=kb.r)
                self.kT_store(kb, 128, half * 8, self.ktscr_s, blk * 128, self.r_scr_s[blk // 4])

    def build(self):
        self.declare()
        self._sb_i = 0
        ntiles_total = self.n_prompt_tiles + (1 if self.do_sample else 0)
        self.plan_weights(ntiles_total)
        stage = getattr(self, "stage", 99)
        self.prologue()
        if self.do_sample and stage >= 1:
            self.sample_cache_prep()
            if stage >= 2:
                self._next_tile = 0 if self.n_prompt_tiles > 0 else None
                self.tile("sample")
        for t in range(self.n_prompt_tiles):
            self._next_tile = t + 1 if t + 1 < self.n_prompt_tiles else None
            self.tile("prompt", t)
        self.S.emit(final_wait_ops=self.fin)
        self.st.close()
        return self.nc


def _consts():
    slopes = 2.0 ** (-8.0 * np.arange(1, H + 1) / H)
    kl = np.arange(128)[:, None, None]
    di = np.arange(NDIST)[None, None, :]
    biasH = slopes[None, :, None] * (kl - 128.0 * (di - 3))
    biasH = biasH.reshape(128, H * NDIST).astype(np.float32)
    klm = np.arange(128)[:, None]
    qq = np.arange(128)[None, :]
    base = np.where(qq >= klm, 0.0, -2.0 * (klm - qq))
    same = (klm // 64) == (qq // 64)
    cm = np.zeros((128, H, 128), np.float64)
    for h in range(H):
        c = slopes[h] * base
        c = np.where((qq < klm) & (~same), -30000.0, c)
        cm[:, h, :] = c
    cmask = cm.reshape(128, H * 128).astype(np.float32)
    qp = np.arange(512)
    qaug = np.zeros((2, 16, 512), np.float64)
    for h in range(H):
        for m in range(2):
            qaug[0, 2 * h + m] = -slopes[h] * 64.0 * (qp // 64)
            qaug[1, 2 * h + m] = -slopes[h] * (qp % 64)
    qaug = qaug.reshape(2, 16 * 512).astype(np.float32)
    s_ = np.arange(128)[:, None]
    t_ = np.arange(128)[None, :]
    tril = (s_ <= t_).astype(np.float32)
    return dict(c_ident=np.eye(128, dtype=np.float32), c_biasH=biasH, c_cmask=cmask, c_qaug=qaug, c_tril=tril)


def _fm(v, n):
    return np.ascontiguousarray(np.asarray(v, np.float32).reshape(n, 128).T)


_NC_CACHE = {}


def kernel(x_prompt, x_sample, cache_k, cache_v, c_prompt, c_sample, ada_w, ada_b, norm_g,
           ffn1_wgu, ffn1_wd, w_in, q_norm_g, k_norm_g, lambda_qk, attn_subln_g, gmlp_vnorm_g,
           gmlp_ws, gmlp_bs, w_gate, b_gate, w_branch, w_out, ffn2_wgu, ffn2_wd,
           _n_prompt_tiles=N_PROMPT_TILES, _do_sample=True, _debug=False, _debug_kind="sample", _stage=99, _ncores=8):
    f = lambda a: np.ascontiguousarray(np.asarray(a, dtype=np.float32))
    x_prompt, x_sample, cache_k, cache_v = f(x_prompt), f(x_sample), f(cache_k), f(cache_v)
    key = (_n_prompt_tiles, _do_sample, _debug, _debug_kind, _stage)
    if key not in _NC_CACHE:
        _NC_CACHE[key] = Builder(_n_prompt_tiles, _do_sample, _debug, _debug_kind, _stage).build()
    nc = _NC_CACHE[key]
    consts = _consts()
    shared = dict(
        ada_w=f(ada_w[0]), ada_bT=_fm(ada_b[0], 72),
        ngT=np.ascontiguousarray(np.asarray(norm_g[0], np.float32).reshape(4, 8, 128).transpose(2, 0, 1).reshape(128, 32)),
        w1gu=f(ffn1_wgu[0]), w1d=f(ffn1_wd[0]), w_in=f(w_in[0]),
        gq_b=np.ascontiguousarray(np.broadcast_to(f(q_norm_g[0])[None, :], (128, 64))),
        gk_b=np.ascontiguousarray(np.broadcast_to(f(k_norm_g[0])[None, :], (128, 64))),
        lq_b=np.ascontiguousarray(np.broadcast_to(f(lambda_qk[0]).reshape(1, 256), (128, 256))),
        sublnT=f(attn_subln_g[0]).reshape(128, 1),
        vng_b=np.ascontiguousarray(np.broadcast_to(f(gmlp_vnorm_g[0])[None, :], (128, D))),
        ws=f(gmlp_ws[0]),
        bs_b=np.ascontiguousarray(np.broadcast_to(f(gmlp_bs[0]).reshape(1, D), (128, D))),
        w_gate=f(w_gate[0]), bgT=_fm(b_gate[0], 16), w_br=f(w_branch[0]), w_out=f(w_out[0]),
        w2gu=f(ffn2_wgu[0]), w2d=f(ffn2_wd[0]),
    )
    shared.update(consts)
    in_maps = []
    for b in range(8):
        m = dict(shared)
        m["xp"] = x_prompt[b]
        m["xs"] = x_sample[b]
        m["ck"] = cache_k[0, b].reshape(PAST, D)
        m["cv"] = cache_v[0, b].reshape(PAST, D)
        cv2 = np.stack([f(c_prompt[b]), f(c_sample[b])], axis=-1)
        m["cvecT"] = np.ascontiguousarray(cv2.reshape(8, 128, 2).transpose(1, 0, 2).reshape(128, 16))
        in_maps.append(m)
    if _ncores != 8:
        in_maps = in_maps[:_ncores]
    res = run_bass_kernel_spmd(nc, in_maps, core_ids=list(range(_ncores)))
    R = list(res.results) + [res.results[0]] * (8 - _ncores)
    if _debug:
        global _DEBUG_RES
        _DEBUG_RES = R
    y_prompt = np.stack([R[b]["yp"] for b in range(8)]).astype(np.float32)
    y_sample = np.stack([R[b]["ys"] for b in range(8)]).astype(np.float32)
    k_prompt = np.stack([R[b]["kp"] for b in range(8)]).reshape(1, 8, SEQ, H, 2, 64).astype(np.float32)
    v_prompt = np.stack([R[b]["vp"] for b in range(8)]).reshape(1, 8, SEQ, H, 128).astype(np.float32)
    k_sample = np.stack([R[b]["ks"] for b in range(8)]).reshape(1, 8, DS, H, 2, 64).astype(np.float32)
    v_sample = np.stack([R[b]["vs"] for b in range(8)]).reshape(1, 8, DS, H, 128).astype(np.float32)
    g_sample = np.stack([R[b]["gs"] for b in range(8)]).reshape(1, 8, DS, D).astype(np.float32)
    return (y_prompt, y_sample, k_prompt, v_prompt, k_sample, v_sample, g_sample)
```
